# Optimizing a Trainium2 kernel written in Bass

```python
import math
import jax, jax.numpy as jnp
from jax import lax
import numpy as np

D_MODEL = 1024
BATCH = 4
SEQ = 4096
DEPTH = 1

HEAD_DIM = 64
D_MIX = D_MODEL
N_GMLP_HEADS = 8
D_GMLP = N_GMLP_HEADS * HEAD_DIM
N_Q_HEADS = 8
N_KV_HEADS = 2
GQA_GROUP = N_Q_HEADS // N_KV_HEADS
D_ATTN = N_Q_HEADS * HEAD_DIM
D_KV = N_KV_HEADS * HEAD_DIM
D_IN = 2 * D_GMLP + D_ATTN + 2 * D_KV
CHUNK = 128
WINDOW = 128
ATTN_BLOCK = 128
ROPE_THETA = 10000.0
D_FF = 4 * D_MODEL
LN_EPS = 1e-5
DEEPNORM_ALPHA = (2.0 * DEPTH) ** 0.25
DEEPNORM_BETA = (8.0 * DEPTH) ** -0.25
NEG_INF = -1e30

kernel_name = "hymba_gmlp_swa_sink_deepnorm"


def layer_norm(x, g, b):
    xf = x.astype(jnp.float32)
    mu = jnp.mean(xf, axis=-1, keepdims=True)
    var = jnp.mean(jnp.square(xf - mu), axis=-1, keepdims=True)
    y = (xf - mu) * lax.rsqrt(var + LN_EPS)
    return (y * g.astype(jnp.float32) + b.astype(jnp.float32)).astype(x.dtype)


def rope(t, positions):
    half = HEAD_DIM // 2
    inv_freq = ROPE_THETA ** (-jnp.arange(0, HEAD_DIM, 2, dtype=jnp.float32) / HEAD_DIM)
    ang = positions.astype(jnp.float32)[..., None] * inv_freq
    cos = jnp.cos(ang)[:, :, None, :]
    sin = jnp.sin(ang)[:, :, None, :]
    tf = t.astype(jnp.float32)
    t1, t2 = tf[..., :half], tf[..., half:]
    out = jnp.concatenate([t1 * cos - t2 * sin, t2 * cos + t1 * sin], axis=-1)
    return out.astype(t.dtype)


def gmlp_mixer(u, v, v_ln_g, v_ln_b, w_spatial, b_spatial):
    B, S, _ = u.shape
    nc = S // CHUNK
    u = jax.nn.gelu(u)
    v = layer_norm(jax.nn.gelu(v), v_ln_g, v_ln_b)
    vc = v.reshape(B, nc, CHUNK, N_GMLP_HEADS, HEAD_DIM)
    causal = jnp.tril(jnp.ones((CHUNK, CHUNK), dtype=w_spatial.dtype))
    w = w_spatial * causal
    mixed = jnp.einsum('hts,bcshd->bcthd', w, vc) + b_spatial.T[None, None, :, :, None]
    out = u.reshape(B, nc, CHUNK, N_GMLP_HEADS, HEAD_DIM) * mixed
    return out.reshape(B, S, D_GMLP)


def swa_sink_attention(q, k, v, positions, sinks):
    B, S, _ = q.shape
    nb = S // ATTN_BLOCK
    q = rope(q.reshape(B, S, N_Q_HEADS, HEAD_DIM), positions)
    k = rope(k.reshape(B, S, N_KV_HEADS, HEAD_DIM), positions)
    v = v.reshape(B, S, N_KV_HEADS, HEAD_DIM)
    qb = q.reshape(B, nb, ATTN_BLOCK, N_KV_HEADS, GQA_GROUP, HEAD_DIM)

    def banded(t):
        tb = t.reshape(B, nb, ATTN_BLOCK, N_KV_HEADS, HEAD_DIM)
        prev = jnp.pad(tb[:, :-1], ((0, 0), (1, 0), (0, 0), (0, 0), (0, 0)))
        return jnp.concatenate([prev, tb], axis=2)

    kb, vb = banded(k), banded(v)
    scores = jnp.einsum('bnqkgd,bnskd->bnkgqs', qb, kb).astype(jnp.float32)
    scores = scores * (1.0 / math.sqrt(HEAD_DIM))

    qi = jnp.arange(ATTN_BLOCK)[:, None]
    si = jnp.arange(2 * ATTN_BLOCK)[None, :]
    dist = qi + ATTN_BLOCK - si
    band = (dist >= 0) & (dist < WINDOW)
    key_abs = jnp.arange(nb)[:, None, None] * ATTN_BLOCK + si[None] - ATTN_BLOCK
    mask = band[None] & (key_abs >= 0)
    scores = jnp.where(mask[None, :, None, None], scores, NEG_INF)

    sink = sinks.astype(jnp.float32).reshape(N_KV_HEADS, GQA_GROUP)
    sink_col = jnp.broadcast_to(sink[None, None, :, :, None, None], scores.shape[:-1] + (1,))
    probs = jax.nn.softmax(jnp.concatenate([scores, sink_col], axis=-1), axis=-1)[..., :-1]
    out = jnp.einsum('bnkgqs,bnskd->bnqkgd', probs.astype(vb.dtype), vb)
    return out.reshape(B, S, D_ATTN)


def setup_inputs(seed: int = 0) -> dict:
    key = jax.random.key(seed)
    ks = jax.random.split(key, 16)
    f32 = jnp.float32
    x = jax.random.normal(ks[0], (BATCH, SEQ, D_MODEL), f32)
    offset = jax.random.randint(ks[1], (BATCH, 1), 0, 1024, dtype=jnp.int32)
    positions = (offset + jnp.arange(SEQ, dtype=jnp.int32)[None, :]).astype(jnp.int32)
    w_in = jax.random.normal(ks[2], (DEPTH, D_MODEL, D_IN), f32) * D_MODEL ** -0.5
    v_ln_g = 1.0 + 0.05 * jax.random.normal(ks[3], (DEPTH, D_GMLP), f32)
    v_ln_b = 0.02 * jax.random.normal(ks[4], (DEPTH, D_GMLP), f32)
    w_spatial = jax.random.normal(ks[5], (DEPTH, N_GMLP_HEADS, CHUNK, CHUNK), f32) * CHUNK ** -0.5
    b_spatial = 1.0 + 0.1 * jax.random.normal(ks[6], (DEPTH, N_GMLP_HEADS, CHUNK), f32)
    sinks = 0.5 * jax.random.normal(ks[7], (DEPTH, N_Q_HEADS), f32)
    w_out = jax.random.normal(ks[8], (DEPTH, D_MIX, D_MODEL), f32) * (D_MIX ** -0.5) * DEEPNORM_BETA
    ln1_g = 1.0 + 0.05 * jax.random.normal(ks[9], (DEPTH, D_MODEL), f32)
    ln1_b = 0.02 * jax.random.normal(ks[10], (DEPTH, D_MODEL), f32)
    w_ff1 = jax.random.normal(ks[11], (DEPTH, D_MODEL, D_FF), f32) * D_MODEL ** -0.5
    w_ff2 = jax.random.normal(ks[12], (DEPTH, D_FF, D_MODEL), f32) * (D_FF ** -0.5) * DEEPNORM_BETA
    ln2_g = 1.0 + 0.05 * jax.random.normal(ks[13], (DEPTH, D_MODEL), f32)
    ln2_b = 0.02 * jax.random.normal(ks[14], (DEPTH, D_MODEL), f32)
    return {"x": x, "positions": positions, "w_in": w_in, "v_ln_g": v_ln_g, "v_ln_b": v_ln_b,
            "w_spatial": w_spatial, "b_spatial": b_spatial, "sinks": sinks, "w_out": w_out,
            "ln1_g": ln1_g, "ln1_b": ln1_b, "w_ff1": w_ff1, "w_ff2": w_ff2,
            "ln2_g": ln2_g, "ln2_b": ln2_b}


def reference(x, positions, w_in, v_ln_g, v_ln_b, w_spatial, b_spatial, sinks, w_out,
              ln1_g, ln1_b, w_ff1, w_ff2, ln2_g, ln2_b):
    split_at = [D_GMLP, 2 * D_GMLP, 2 * D_GMLP + D_ATTN, 2 * D_GMLP + D_ATTN + D_KV]
    for l in range(DEPTH):
        h = x @ w_in[l]
        u, v_g, q, k, v_a = jnp.split(h, split_at, axis=-1)
        a_out = gmlp_mixer(u, v_g, v_ln_g[l], v_ln_b[l], w_spatial[l], b_spatial[l])
        b_out = swa_sink_attention(q, k, v_a, positions, sinks[l])
        mix = jnp.concatenate([a_out, b_out], axis=-1) @ w_out[l]
        x = layer_norm(DEEPNORM_ALPHA * x + mix, ln1_g[l], ln1_b[l])
        ff = jnp.square(jax.nn.relu(x @ w_ff1[l])) @ w_ff2[l]
        x = layer_norm(DEEPNORM_ALPHA * x + ff, ln2_g[l], ln2_b[l])
    return x
```

```python
import math
from contextlib import ExitStack

import numpy as np
import concourse.bass as bass
import concourse.mybir as mybir
from concourse.bass_utils import run_bass_kernel_spmd

F32 = mybir.dt.float32
BF16 = mybir.dt.bfloat16
I32 = mybir.dt.int32
AF = mybir.ActivationFunctionType
ALU = mybir.AluOpType

NCORES = 8
T = 2048
NB = T // 128
D = 1024
DIN = 1792
DFF = 4096
G = 512
NG = DFF // G
ALPHA = (2.0 * 1) ** 0.25
EPS = 1e-5
NSLOT = 2


class Sched:
    def __init__(self, nc, sems, dma_sems):
        self.nc = nc
        self.E = {"pe": nc.tensor, "act": nc.scalar, "dve": nc.vector, "pool": nc.gpsimd, "sp": nc.sync}
        self.sem = sems
        self.cnt = {k: 0 for k in sems}
        self.dsem = dma_sems
        self.dval = [0] * len(dma_sems)
        self.dnext = {"sp": 0, "pool": 0}
        self.waited = {e: {} for e in self.E}
        self.lastw = {}
        self.readers = {}
        self.extra = {}

    def _deps(self, reads, writes):
        need = {}

        def add(k, v):
            if need.get(k, 0) < v:
                need[k] = v

        for r in list(reads) + list(writes):
            for k, v in self.extra.get(r, {}).items():
                add(k, v)
        for r in reads:
            t = self.lastw.get(r)
            if t:
                add(*t)
        for w in writes:
            t = self.lastw.get(w)
            if t:
                add(*t)
            for k, v in self.readers.get(w, {}).items():
                add(k, v)
        return need

    def _wait(self, e, need):
        for k, v in need.items():
            if k == "pe" and e == "pe":
                continue
            if self.waited[e].get(k, 0) >= v:
                continue
            h = self.sem[k] if isinstance(k, str) else self.dsem[k]
            self.E[e].wait_ge(h, v)
            self.waited[e][k] = v

    def _commit(self, tok, reads, writes):
        k, v = tok
        for r in reads:
            d = self.readers.setdefault(r, {})
            if d.get(k, 0) < v:
                d[k] = v
        for w in writes:
            self.lastw[w] = tok
            self.readers[w] = {}

    def op(self, e, reads, writes, fn):
        self._wait(e, self._deps(reads, writes))
        ins = fn(self.E[e])
        self.cnt[e] += 1
        ins.then_inc(self.sem[e], 1)
        self._commit((e, self.cnt[e]), reads, writes)

    def dma(self, q, out, in_, reads, writes):
        half = len(self.dsem) // 2
        base = 0 if q == "sp" else half
        i = base + self.dnext[q] % half
        self.dnext[q] += 1
        need = self._deps(reads, writes)
        if self.dval[i] > 0 and need.get(i, 0) < self.dval[i]:
            need[i] = self.dval[i]
        self._wait(q, need)
        self.E[q].dma_start(out=out, in_=in_).then_inc(self.dsem[i], 16)
        self.dval[i] += 16
        self._commit((i, self.dval[i]), reads, writes)

    def alias(self, new_res, old_res):
        need = self._deps([], old_res)
        for r in new_res:
            d = self.extra.setdefault(r, {})
            for k, v in need.items():
                if d.get(k, 0) < v:
                    d[k] = v

    def drain(self, q):
        need = {i: v for i, v in enumerate(self.dval) if v > 0}
        self._wait(q, need)


def build_program():
    nc = bass.Bass("TRN2", target_bir_lowering=False)

    def din(name, shape, dt=F32):
        return nc.dram_tensor(name, shape, dt, kind="ExternalInput").ap()

    x_d = din("x", [T, D])
    xT_d = din("xT", [D, T + 128])
    pos_d = din("pos", [128, NB + 1], I32)
    win_d = din("w_in", [D, DIN])
    wout_d = din("w_out", [D, D])
    wf1_d = din("w_ff1", [D, DFF])
    wf2_d = din("w_ff2", [DFF, D])
    wsp_d = din("w_spT", [128, 8, 128])
    bT_d = din("bT", [128, 4, 128])
    l1g_d = din("l1g", [128, D])
    l1b_d = din("l1b", [128, D])
    l2g_d = din("l2g", [128, D])
    l2b_d = din("l2b", [128, D])
    esk_d = din("esk", [128, 8])
    idn_d = din("ident", [128, 128])
    msk_d = din("masks", [128, 2, 2, 128])
    invf_d = din("invf", [128, 32])
    invl_d = din("invf_lo", [128, 32])
    mb_d = din("mbias", [128, 2, 2, 512])
    gcol_d = din("gcol", [128, 4])
    g1c_d = din("g1col", [128, 8])
    b1c_d = din("b1col", [128, 8])
    bcol_d = din("bcol", [128, 4])
    out_d = nc.dram_tensor("out", [T, D], F32, kind="ExternalOutput").ap()

    es = ExitStack()
    with es:
        sems = {k: es.enter_context(nc.semaphore("sem_" + k)) for k in ("pe", "act", "dve", "pool")}
        dsems = [es.enter_context(nc.semaphore("semd%d" % i)) for i in range(16)]
        S = Sched(nc, sems, dsems)

        def sb(name, shape, dt):
            return es.enter_context(nc.sbuf_tensor("sb_" + name, shape, dt))

        def pst(name, shape, dt):
            return es.enter_context(nc.psum_tensor("pp_" + name, shape, dt))

        x1 = sb("x1", [128, NB, D], F32)
        x1T = sb("x1T", [128, 8, T], BF16)
        NSS = NB
        st6 = sb("st6", [128, NSS, 2, 6], F32)
        mv = sb("mv", [128, NSS, 2], F32)
        ve = sb("ve", [128, NSS], F32)
        rstd = sb("rstd", [128, NSS], F32)
        mhalf = sb("mhalf", [128, 1], F32)
        nmr = sb("nmr", [128, NSS], F32)

        arenaW = sb("arenaW", [128, 8 * DIN], BF16)
        Win = arenaW[:, :].rearrange("p (k n) -> p k n", k=8)
        Wout = sb("Wout", [128, 8, D], BF16)
        xTh = sb("xTh", [128, 8, 128], BF16)
        uT = sb("uT", [128, 4, 512], F32)
        ident = sb("ident", [128, 128], BF16)
        masks = sb("masks", [128, 2, 2, 128], BF16)
        invf = sb("invf", [128, 32], F32)
        invl = sb("invl", [128, 32], F32)
        posi = sb("posi", [128, NB + 1], I32)
        posf = sb("posf", [128, NB + 1], F32)
        Ct = sb("Ct", [128, NB + 1, 32], F32)
        St = sb("St", [128, NB + 1, 32], F32)
        WcT = sb("WcT", [128, 8, 128], BF16)
        bT = sb("bT", [128, 4, 128], F32)
        mbias = sb("mbias", [128, 2, 2, 512], BF16)
        gcol = sb("gcol", [128, 4], F32)
        g1col = sb("g1col", [128, 8], F32)
        b1col = sb("b1col", [128, 8], F32)
        bcol = sb("bcol", [128, 4], F32)
        ones = sb("ones", [128, 128], BF16)
        st6v = sb("st6v", [128, 1, 6], F32)
        mvv = sb("mvv", [128, 2], F32)
        vev = sb("vev", [128, 1], F32)
        rstdv = sb("rstdv", [128, 1], F32)
        l1g = sb("l1g", [128, D], F32)
        l1b = sb("l1b", [128, D], F32)
        esk = sb("esk", [128, 8], F32)
        esink = sb("esink", [128, 8], F32)
        vg = sb("vg", [128, 512], F32)
        vln = sb("vln", [128, 2, 512], BF16)
        qk = sb("qk", [128, 640], F32)
        rt = sb("rt", [128, 3, 2, 128], F32)
        qkr = sb("qkr", [128, 2, 640], BF16)
        qT = sb("qT", [128, 512], BF16)
        kT = sb("kT", [128, 2, 128], BF16)
        vext = sb("vext", [128, 4, 2, 65], BF16)
        PT = sb("PT", [128, 4, 512], BF16)
        den = sb("den", [128, 8], F32)
        rden = sb("rden", [128, 8], F32)
        oatt = sb("oatt", [128, 512], BF16)
        catT = sb("catT", [128, 3, 8, 128], BF16)
        tmpa = sb("tmpa", [128, 512], F32)
        x1b = sb("x1b", [128, D], BF16)
        ps = [pst("ps%d" % i, [128, 512], F32) for i in range(7)]
        psT = pst("psT", [128, 1024], BF16)

        NP = (NB + 1) * 32
        uflat = uT[:].rearrange("p j t -> p (j t)")
        ang = uflat[:, 0:NP].rearrange("p (b d) -> p b d", b=NB + 1)
        kf = uflat[:, NP:2 * NP].rearrange("p (b d) -> p b d", b=NB + 1)
        mk = uflat[:, 2 * NP:3 * NP].rearrange("p (b d) -> p b d", b=NB + 1)
        r2 = qk[:, 0:NP].rearrange("p (b d) -> p b d", b=NB + 1)
        wspb = x1b[:].rearrange("p (h t) -> p h t", h=8)
        ANG = [("uT", 0), ("uT", 1)]
        KF = [("uT", 1), ("uT", 2)]
        MK = [("uT", 2), ("uT", 3)]
        R2 = ["qk_q", "qk_k"]

        def ln_stats(buf, res, si):
            def f(e):
                last = None
                for h in range(2):
                    last = e.bn_stats(st6[:, si, h, :], buf[:, h * 512:(h + 1) * 512])
                return last
            S.op("dve", [res], [("st6", si)], f)
            S.op("dve", [("st6", si)], [("mv", si)], lambda e: e.bn_aggr(mv[:, si, :], st6[:, si, :, :]))
            S.op("dve", [("mv", si)], [("ve", si)], lambda e: e.tensor_scalar(
                out=ve[:, si:si + 1], in0=mv[:, si, 1:2], scalar1=EPS, scalar2=None, op0=ALU.add))
            S.op("pool", [("ve", si), "mhalf"], [("rstd", si)], lambda e: e.tensor_tensor(
                out=rstd[:, si:si + 1], in0=ve[:, si:si + 1], in1=mhalf[:], op=ALU.pow))

        def ln_apply(buf, gt, bt, res, g_eng, n_eng, si):
            if n_eng == "dve":
                S.op("dve", [res, ("mv", si), ("rstd", si)], [res], lambda e: e.tensor_scalar(
                    out=buf, in0=buf, scalar1=mv[:, si, 0:1], scalar2=rstd[:, si:si + 1],
                    op0=ALU.subtract, op1=ALU.mult))
            else:
                S.op("dve", [("mv", si), ("rstd", si)], [("nmr", si)], lambda e: e.scalar_tensor_tensor(
                    out=nmr[:, si:si + 1], in0=mv[:, si, 0:1], scalar=-1.0, in1=rstd[:, si:si + 1],
                    op0=ALU.mult, op1=ALU.mult))
                S.op("act", [res, ("nmr", si), ("rstd", si)], [res], lambda e: e.activation(
                    out=buf, in_=buf, func=AF.Identity, bias=nmr[:, si:si + 1], scale=rstd[:, si:si + 1]))
            S.op(g_eng, [res, gt[1]], [res], lambda e: e.tensor_tensor(
                out=buf, in0=buf, in1=gt[0], op=ALU.mult))
            S.op("pool", [res, bt[1]], [res], lambda e: e.tensor_tensor(
                out=buf, in0=buf, in1=bt[0], op=ALU.add))

        S.op("dve", [], ["mhalf"], lambda e: e.memset(mhalf[:], -0.5))

        xT_v = xT_d.rearrange("(kc p) t -> p kc t", p=128)
        win_v = win_d.rearrange("(kc p) n -> p kc n", p=128)
        wout_v = wout_d.rearrange("(kc p) n -> p kc n", p=128)
        S.dma("pool", masks[:], msk_d, [], ["masks"])
        S.dma("pool", wspb, wsp_d, [], ["x1b"])
        S.dma("pool", xTh[:], xT_v[:, :, 0:128], [], ["xTh"])
        S.dma("pool", x1T[:, :, 0:128], xT_v[:, :, 128:256], [], [("XT", 0)])
        for kc in range(8):
            S.dma("pool", Win[:, kc, 512:DIN], win_v[:, kc, 512:DIN], [], [("Win", kc)])
        S.dma("pool", ident[:], idn_d, [], ["ident"])
        S.dma("pool", x1T[:, :, 128:256], xT_v[:, :, 256:384], [], [("XT", 1)])
        S.dma("sp", posi[:], pos_d, [], ["posi"])
        S.dma("sp", invf[:], invf_d, [], ["invf"])
        S.dma("sp", invl[:], invl_d, [], ["invl"])
        S.dma("sp", esk[:], esk_d, [], ["esk"])
        S.dma("sp", gcol[:], gcol_d, [], ["gcol"])
        S.dma("sp", g1col[:], g1c_d, [], ["g1col"])
        S.dma("sp", b1col[:], b1c_d, [], ["b1col"])
        S.dma("sp", bcol[:], bcol_d, [], ["bcol"])
        S.dma("sp", bT[:], bT_d, [("Win", 7)], ["bT"])

        def x_load(b):
            S.dma("sp", x1[:, b, :], x_d[b * 128:(b + 1) * 128, :], [("Win", 7), "tick"], [("x1", b)])

        S.dma("pool", mbias[:], mb_d, [("Win", 7)], ["mbias"])
        S.dma("pool", x1T[:, :, 256:512], xT_v[:, :, 384:128 + 512], [], [("XT", 2), ("XT", 3)])
        for kc in range(8):
            S.dma("pool", Win[:, kc, 0:512], win_v[:, kc, 0:512], [], [("Wu", kc)])

        def late_loads(it):
            if it == 0:
                for kc in range(8):
                    S.dma("pool", Wout[:, kc, :], wout_v[:, kc, :], [], [("Wout", kc)])
                S.dma("sp", l1g[:], l1g_d, ["tick"], ["l1g"])
                S.dma("sp", l1b[:], l1b_d, ["tick"], ["l1b"])
            if it == 1:
                for gi in range(1, 4):
                    S.dma("pool", x1T[:, :, gi * 512:(gi + 1) * 512],
                          xT_v[:, :, 128 + gi * 512:128 + (gi + 1) * 512], [], [("XT", 4 * gi + j) for j in range(4)])
            if it + 2 < NB:
                x_load(it + 2)

        x_load(0)
        x_load(1)
        WIN = [("Win", kc) for kc in range(8)]
        WU = [("Wu", kc) for kc in range(8)]
        WOUT = [("Wout", kc) for kc in range(8)]

        TWO_PI = 2.0 * math.pi
        C1 = 6.28125
        C2 = TWO_PI - C1
        MAGIC = 12582912.0
        PI_SAFE = 3.141592
        S.op("dve", ["posi"], ["posf"], lambda e: e.tensor_copy(posf[:], posi[:]))
        S.op("dve", ["posf", "invf"], ANG, lambda e: e.tensor_tensor(
            out=ang, in0=invf[:].unsqueeze(1).broadcast_to([128, NB + 1, 32]),
            in1=posf[:].unsqueeze(2).broadcast_to([128, NB + 1, 32]), op=ALU.mult))

        def fold(buf, tags):
            S.op("dve", tags, MK, lambda e: e.tensor_single_scalar(
                out=mk, in_=buf, scalar=math.pi, op=ALU.is_gt))
            S.op("dve", tags + MK, tags, lambda e: e.scalar_tensor_tensor(
                out=buf, in0=mk, scalar=-TWO_PI, in1=buf, op0=ALU.mult, op1=ALU.add))
            S.op("dve", tags, MK, lambda e: e.tensor_single_scalar(
                out=mk, in_=buf, scalar=-math.pi, op=ALU.is_lt))
            S.op("dve", tags + MK, tags, lambda e: e.scalar_tensor_tensor(
                out=buf, in0=mk, scalar=TWO_PI, in1=buf, op0=ALU.mult, op1=ALU.add))
            S.op("dve", tags, tags, lambda e: e.tensor_scalar(
                out=buf, in0=buf, scalar1=PI_SAFE, scalar2=-PI_SAFE, op0=ALU.min, op1=ALU.max))

        S.op("dve", ANG, KF, lambda e: e.tensor_scalar(
            out=kf, in0=ang, scalar1=1.0 / TWO_PI, scalar2=MAGIC, op0=ALU.mult, op1=ALU.add))
        S.op("dve", KF, KF, lambda e: e.tensor_scalar(
            out=kf, in0=kf, scalar1=-MAGIC, scalar2=None, op0=ALU.add))
        S.op("dve", ANG + KF, ANG, lambda e: e.scalar_tensor_tensor(
            out=ang, in0=kf, scalar=-C1, in1=ang, op0=ALU.mult, op1=ALU.add))
        S.op("dve", ANG + KF, ANG, lambda e: e.scalar_tensor_tensor(
            out=ang, in0=kf, scalar=-C2, in1=ang, op0=ALU.mult, op1=ALU.add))
        S.op("dve", ["posf", "invl"], MK, lambda e: e.tensor_tensor(
            out=mk, in0=invl[:].unsqueeze(1).broadcast_to([128, NB + 1, 32]),
            in1=posf[:].unsqueeze(2).broadcast_to([128, NB + 1, 32]), op=ALU.mult))
        S.op("dve", ANG + MK, ANG, lambda e: e.tensor_tensor(out=ang, in0=ang, in1=mk, op=ALU.add))
        S.op("dve", ANG, R2, lambda e: e.tensor_scalar(
            out=r2, in0=ang, scalar1=0.5 * math.pi, scalar2=None, op0=ALU.add))
        fold(ang, ANG)
        fold(r2, R2)
        S.op("act", ANG, ["St"], lambda e: e.activation(out=St[:], in_=ang, func=AF.Sin))
        S.op("act", R2, ["Ct"], lambda e: e.activation(out=Ct[:], in_=r2, func=AF.Sin))
        S.op("act", ["esk"], ["esink"], lambda e: e.activation(out=esink[:], in_=esk[:], func=AF.Exp))
        S.op("dve", ["x1b", "masks"], ["WcT"], lambda e: e.tensor_tensor(
            out=WcT[:], in0=wspb, in1=masks[:, 1, 1, :].unsqueeze(1).broadcast_to([128, 8, 128]),
            op=ALU.mult))
        S.op("dve", [], [("vext", i) for i in range(4)], lambda e: e.memset(vext[:], 1.0))

        S.op("dve", [], ["ones"], lambda e: e.memset(ones[:], 1.0))

        csT = tmpa[:].rearrange("p (j t) -> p j t", j=4)
        for half in range(2):
            S.op("pe", ["WcT", "ones"], [("ps", half)], lambda e, half=half: e.matmul(
                ps[half][:, :], ones[:], WcT[:, 4 * half:4 * half + 4, :].rearrange("p h t -> p (h t)"),
                start=True, stop=True))
            pv = ps[half][:, :].rearrange("p (j hp t) -> p j hp t", j=2, hp=2)
            for hp in range(2):
                S.op("dve", [("ps", half)], ["tmpa"], lambda e, half=half, hp=hp, pv=pv: e.tensor_copy(
                    csT[hp * 64:(hp + 1) * 64, 2 * half:2 * half + 2, :], pv[hp * 64:(hp + 1) * 64, :, hp, :]))
        S.op("dve", ["tmpa", "bcol"], ["tmpa"], lambda e: e.tensor_tensor(
            out=csT, in0=csT, in1=bcol[:].unsqueeze(2).broadcast_to([128, 4, 128]), op=ALU.mult))
        S.op("dve", ["tmpa", "bT"], ["bT"], lambda e: e.tensor_tensor(out=bT[:], in0=bT[:], in1=csT, op=ALU.add))

        def rope_ops(eng, bidx, src1, src2, dst1, dst2, shape, rd, wr, ti):
            cb = Ct[:, bidx, :].unsqueeze(1).broadcast_to(shape)
            sbv = St[:, bidx, :].unsqueeze(1).broadcast_to(shape)
            n = shape[1] * shape[2]
            ta = rt[:, ti, 0, 0:n].rearrange("p (a d) -> p a d", a=shape[1])
            tb = rt[:, ti, 1, 0:n].rearrange("p (a d) -> p a d", a=shape[1])
            r0, r1 = ("rt0", ti), ("rt1", ti)
            S.op(eng, rd + ["Ct"], [r0], lambda e: e.tensor_tensor(out=ta, in0=src1, in1=cb, op=ALU.mult))
            S.op(eng, rd + ["St"], [r1], lambda e: e.tensor_tensor(out=tb, in0=src2, in1=sbv, op=ALU.mult))
            S.op(eng, [r0, r1], wr, lambda e: e.tensor_tensor(out=dst1, in0=ta, in1=tb, op=ALU.subtract))
            S.op(eng, rd + ["Ct"], [r0], lambda e: e.tensor_tensor(out=ta, in0=src2, in1=cb, op=ALU.mult))
            S.op(eng, rd + ["St"], [r1], lambda e: e.tensor_tensor(out=tb, in0=src1, in1=sbv, op=ALU.mult))
            S.op(eng, [r0, r1], wr, lambda e: e.tensor_tensor(out=dst2, in0=ta, in1=tb, op=ALU.add))

        def rope_q(bidx, qs):
            qs4 = qk[:, 0:512].rearrange("p (kv c two d) -> p kv c two d", kv=2, c=4, two=2)
            qd4 = qkr[:, qs, 0:512].rearrange("p (c kv two d) -> p kv c two d", kv=2, c=4, two=2)
            shape = [128, 2, 4, 32]
            cb = Ct[:, bidx, :].unsqueeze(1).unsqueeze(1).broadcast_to(shape)
            sbv = St[:, bidx, :].unsqueeze(1).unsqueeze(1).broadcast_to(shape)
            ta = rt[:, 0:2, 0, :].rearrange("p a (c d) -> p a c d", c=4)
            tb = rt[:, 0:2, 1, :].rearrange("p a (c d) -> p a c d", c=4)
            s1, s2 = qs4[:, :, :, 0, :], qs4[:, :, :, 1, :]
            d1, d2 = qd4[:, :, :, 0, :], qd4[:, :, :, 1, :]
            wr = [("qkr_q", qs, 0), ("qkr_q", qs, 1)]
            r0 = [("rt0", 0), ("rt0", 1)]
            r1 = [("rt1", 0), ("rt1", 1)]
            S.op("dve", ["qk_q", "Ct"], r0, lambda e: e.tensor_tensor(out=ta, in0=s1, in1=cb, op=ALU.mult))
            S.op("dve", ["qk_q", "St"], r1, lambda e: e.tensor_tensor(out=tb, in0=s2, in1=sbv, op=ALU.mult))
            S.op("dve", r0 + r1, wr, lambda e: e.tensor_tensor(out=d1, in0=ta, in1=tb, op=ALU.subtract))
            S.op("dve", ["qk_q", "Ct"], r0, lambda e: e.tensor_tensor(out=ta, in0=s2, in1=cb, op=ALU.mult))
            S.op("dve", ["qk_q", "St"], r1, lambda e: e.tensor_tensor(out=tb, in0=s1, in1=sbv, op=ALU.mult))
            S.op("dve", r0 + r1, wr, lambda e: e.tensor_tensor(out=d2, in0=ta, in1=tb, op=ALU.add))

        def rope_k(bidx, qs):
            qsrc = qk[:, 512:640].rearrange("p (h two d) -> p h two d", h=2, two=2)
            kdst = qkr[:, qs, 512:640].rearrange("p (h two d) -> p h two d", h=2, two=2)
            rope_ops("dve" if bidx <= 2 else "pool", bidx, qsrc[:, :, 0, :], qsrc[:, :, 1, :], kdst[:, :, 0, :], kdst[:, :, 1, :],
                     [128, 2, 32], ["qk_k"], [("qkr_k", qs)], 2)

        def halo():
            def f(e):
                last = None
                for kc in range(8):
                    last = e.matmul(ps[2][:, 0:256], xTh[:, kc, :], Win[:, kc, 1536:1792],
                                    start=(kc == 0), stop=(kc == 7))
                return last
            S.op("pe", WIN + ["xTh"], [("ps", 2)], f)
            S.op("act", [("ps", 2)], ["qk_k"], lambda e: e.copy(out=qk[:, 512:640], in_=ps[2][:, 0:128]))
            S.op("act", [("ps", 2)], [("vext", 3)], lambda e: e.copy(
                out=vext[:, 3, :, 0:64], in_=ps[2][:, 128:256].rearrange("p (h d) -> p h d", h=2)))
            rope_k(0, 1)

        def halo_b():
            S.op("pe", [("qkr_k", 1), "ident"], [("ps", 7)], lambda e: e.transpose(
                psT[:, 512:640], qkr[:, 1, 512:640], ident[:]))
            S.op("act", [("ps", 7)], [("kT", 1)], lambda e: e.copy(out=kT[:, 1, :], in_=psT[:, 512:640]))

        def step_U(gi):
            for j in range(4):
                def f(e, j=j):
                    last = None
                    for kc in range(8):
                        last = e.matmul(ps[3 + j][:, :], Win[:, kc, j * 128:(j + 1) * 128],
                                        x1T[:, kc, gi * 512:(gi + 1) * 512], start=(kc == 0), stop=(kc == 7))
                    return last
                S.op("pe", WU + [("XT", 4 * gi + i) for i in range(4)], [("ps", 3 + j)], f)
                S.op("act", [("ps", 3 + j)], [("uT", j)], lambda e, j=j: e.activation(
                    out=uT[:, j, :], in_=ps[3 + j][:, :], func=AF.Gelu_apprx_tanh))

        def A_pe(b):
            tc = b * 128
            vs = b % 4

            def f(e):
                last = None
                for kc in range(8):
                    l = x1T[:, kc, tc:tc + 128]
                    e.matmul(ps[0][:, :], l, Win[:, kc, 512:1024], start=(kc == 0), stop=(kc == 7))
                    e.matmul(ps[1][:, :], l, Win[:, kc, 1024:1536], start=(kc == 0), stop=(kc == 7))
                    last = e.matmul(ps[2][:, 0:256], l, Win[:, kc, 1536:1792], start=(kc == 0), stop=(kc == 7))
                return last
            S.op("pe", WIN + [("XT", b)], [("ps", 0), ("ps", 1), ("ps", 2)], f)
            S.op("dve", [("ps", 2)], ["qk_k"], lambda e: e.tensor_copy(qk[:, 512:640], ps[2][:, 0:128]))
            S.op("dve", [("ps", 2)], [("vext", vs)], lambda e: e.tensor_copy(
                vext[:, vs, :, 0:64], ps[2][:, 128:256].rearrange("p (h d) -> p h d", h=2)))
            S.op("dve", [("ps", 1)], ["qk_q"], lambda e: e.tensor_copy(qk[:, 0:512], ps[1][:, :]))
            S.op("act", [("ps", 0)], ["vg"], lambda e: e.activation(
                out=vg[:], in_=ps[0][:, :], func=AF.Gelu_apprx_tanh))

        def A_vstats(b):
            S.op("dve", ["vg"], ["st6v"], lambda e: e.bn_stats(st6v[:, 0, :], vg[:]))
            S.op("dve", ["st6v"], ["mvv"], lambda e: e.bn_aggr(mvv[:], st6v[:, 0:1, :]))
            S.op("dve", ["mvv"], ["vev"], lambda e: e.tensor_scalar(
                out=vev[:], in0=mvv[:, 1:2], scalar1=EPS, scalar2=None, op0=ALU.add))
            S.op("pool", ["vev", "mhalf"], ["rstdv", "tick"], lambda e: e.tensor_tensor(
                out=rstdv[:], in0=vev[:], in1=mhalf[:], op=ALU.pow))

        def A_vtail(b):
            s2 = b % 2
            S.op("dve", ["vg", "mvv", "rstdv"], [("vln", s2)], lambda e: e.tensor_scalar(
                out=vln[:, s2, :], in0=vg[:], scalar1=mvv[:, 0:1], scalar2=rstdv[:, 0:1],
                op0=ALU.subtract, op1=ALU.mult))

        def B_pe(b):
            bi = b % 4
            s2 = b % 2

            def f(e):
                last = None
                for h in range(8):
                    j, hp = divmod(h, 2)
                    last = e.matmul(ps[2][hp * 64:(hp + 1) * 64, j * 128:(j + 1) * 128],
                                    vln[:, s2, h * 64:(h + 1) * 64], WcT[:, h, :], start=True, stop=True)
                return last
            S.op("pe", [("vln", s2), "WcT"], [("ps", 2)], f)
            for j in range(4):
                S.op("act", [("ps", 2), "gcol"], [("tmpa", j)], lambda e, j=j: e.activation(
                    out=tmpa[:, j * 128:(j + 1) * 128], in_=ps[2][:, j * 128:(j + 1) * 128],
                    func=AF.Identity, scale=gcol[:, j:j + 1]))

        def B_pool(b):
            bi = b % 4
            s2 = b % 3
            TA = [("tmpa", j) for j in range(4)]
            S.op("pool", TA + ["bT"], TA, lambda e: e.tensor_tensor(
                out=tmpa[:], in0=tmpa[:], in1=bT[:].rearrange("p j t -> p (j t)"), op=ALU.add))
            S.op("pool", TA + [("uT", j) for j in range(4)], [("catT_a", s2)], lambda e: e.tensor_tensor(
                out=catT[:, s2, 0:4, :], in0=tmpa[:].rearrange("p (j t) -> p j t", j=4),
                in1=uT[:, :, bi * 128:(bi + 1) * 128], op=ALU.mult))

        def step_C(b):
            s2 = b % 2

            def f(e):
                last = None
                for j in range(5):
                    last = e.transpose(psT[:, j * 128:(j + 1) * 128], qkr[:, s2, j * 128:(j + 1) * 128], ident[:])
                return last
            S.op("pe", [("qkr_q", s2, 0), ("qkr_q", s2, 1), ("qkr_k", s2), "ident"], [("ps", 7)], f)
            S.op("act", [("ps", 7)], ["qT"], lambda e: e.copy(out=qT[:], in_=psT[:, 0:512]))
            S.op("act", [("ps", 7)], [("kT", s2)], lambda e: e.copy(out=kT[:, s2, :], in_=psT[:, 512:640]))

        def D_half(b, kv):
            cur = b % 2
            prv = 1 - cur
            mi = 0 if b == 0 else 1

            def f(e):
                last = None
                for kb, slot in ((0, prv), (1, cur)):
                    o = ps[3 + kv * 2 + kb][:, :]
                    e.matmul(o, kT[kv * 64:(kv + 1) * 64, slot, :], qT[kv * 64:(kv + 1) * 64, :],
                             start=True, stop=False)
                    last = e.matmul(o, ident[:], mbias[:, mi, kb, :], start=False, stop=True)
                return last
            S.op("pe", ["qT", ("kT", 0), ("kT", 1), "ident", "mbias"],
                 [("ps", 3 + kv * 2), ("ps", 4 + kv * 2)], f)

        def D_all(b):
            cur = b % 2
            prv = 1 - cur
            mi = 0 if b == 0 else 1

            def f(e):
                last = None
                for kb, slot in ((0, prv), (1, cur)):
                    for kv in range(2):
                        e.matmul(ps[3 + kv * 2 + kb][:, :], kT[kv * 64:(kv + 1) * 64, slot, :],
                                 qT[kv * 64:(kv + 1) * 64, :], start=True, stop=False)
                for kv in range(2):
                    for kb in range(2):
                        last = e.matmul(ps[3 + kv * 2 + kb][:, :], ident[:], mbias[:, mi, kb, :],
                                        start=False, stop=True)
                return last
            S.op("pe", ["qT", ("kT", 0), ("kT", 1), "ident", "mbias"],
                 [("ps", 3), ("ps", 4), ("ps", 5), ("ps", 6)], f)

        def D_exp(b, kv):
            for kb in range(2):
                jj = kv * 2 + kb
                S.op("act", [("ps", 3 + jj)], [("PT", jj)], lambda e, jj=jj: e.activation(
                    out=PT[:, jj, :], in_=ps[3 + jj][:, :], func=AF.Exp, scale=0.125))

        def step_E(b):
            cur = b % 4
            prv = (b - 1) % 4

            def f(e):
                last = None
                for kv in range(2):
                    o = ps[5 + kv][:, 0:260].rearrange("p (c d) -> p c d", c=4)
                    for c in range(4):
                        for kb, slot in ((0, prv), (1, cur)):
                            last = e.matmul(o[:, c, :], PT[:, kv * 2 + kb, c * 128:(c + 1) * 128],
                                            vext[:, slot, kv, :], start=(kb == 0), stop=(kb == 1))
                return last
            S.op("pe", [("PT", j) for j in range(4)] + [("vext", prv), ("vext", cur)], [("ps", 5), ("ps", 6)], f)
            for kv in range(2):
                o = ps[5 + kv][:, 0:260].rearrange("p (c d) -> p c d", c=4)
                S.op("dve", [("ps", 5 + kv), "esink"], ["den"], lambda e, kv=kv, o=o: e.tensor_tensor(
                    out=den[:, kv * 4:(kv + 1) * 4], in0=o[:, :, 64], in1=esink[:, kv * 4:(kv + 1) * 4],
                    op=ALU.add))
            S.op("dve", ["den"], ["rden"], lambda e: e.reciprocal(out=rden[:], in_=den[:]))
            for kv in range(2):
                o = ps[5 + kv][:, 0:260].rearrange("p (c d) -> p c d", c=4)
                S.op("dve", [("ps", 5 + kv), "rden"], ["oatt"], lambda e, kv=kv, o=o: e.tensor_tensor(
                    out=oatt[:, kv * 256:(kv + 1) * 256].rearrange("p (c d) -> p c d", c=4),
                    in0=o[:, :, 0:64],
                    in1=rden[:, kv * 4:(kv + 1) * 4].unsqueeze(2).broadcast_to([128, 4, 64]), op=ALU.mult))

        def F_pe(b):
            s2 = b % 2

            def f(e):
                last = None
                for j in range(4):
                    last = e.transpose(psT[:, j * 128:(j + 1) * 128], oatt[:, j * 128:(j + 1) * 128], ident[:])
                return last
            S.op("pe", ["oatt", "ident"], [("ps", 7)], f)

        def F_evac(b):
            s2 = b % 3
            S.op("act", [("ps", 7)], [("catT_b", s2)], lambda e: e.copy(
                out=catT[:, s2, 4:8, :], in_=psT[:, 0:512].rearrange("p (j t) -> p j t", j=4)))

        def G_pe(b):
            s2 = b % 3

            def f(e):
                last = None
                for kc in range(8):
                    for hf in range(2):
                        last = e.matmul(ps[hf][:, :], catT[:, s2, kc, :], Wout[:, kc, hf * 512:(hf + 1) * 512],
                                        start=(kc == 0), stop=(kc == 7))
                return last
            S.op("pe", WOUT + [("catT_a", s2), ("catT_b", s2)], [("ps", 0), ("ps", 1)], f)
            xb = x1[:, b, :]
            for hf in range(2):
                S.op("dve", [("ps", hf), ("x1", b)], [("x1", b)], lambda e, hf=hf: e.scalar_tensor_tensor(
                    out=xb[:, hf * 512:(hf + 1) * 512], in0=xb[:, hf * 512:(hf + 1) * 512], scalar=ALPHA,
                    in1=ps[hf][:, :], op0=ALU.mult, op1=ALU.add))
            ln_stats(xb, ("x1", b), b % NSS)

        def G_tail(b):
            xb = x1[:, b, :]
            si = b % NSS
            S.op("dve", [("x1", b), ("mv", si), ("rstd", si)], ["x1b"], lambda e: e.tensor_scalar(
                out=x1b[:], in0=xb, scalar1=mv[:, si, 0:1], scalar2=rstd[:, si:si + 1],
                op0=ALU.subtract, op1=ALU.mult))

        def G_late_act(b):
            xb = x1[:, b, :]
            si = b % NSS
            S.op("dve", [("x1", b), ("mv", si), "l1g"], [("x1", b)], lambda e: e.scalar_tensor_tensor(
                out=xb, in0=xb, scalar=mv[:, si, 0:1], in1=l1g[:], op0=ALU.subtract, op1=ALU.mult))

        def ffn_init(b):
            xb = x1[:, b, :]
            si = b % NSS
            S.op("dve", [("x1", b), ("rstd", si), "l1b"], [("x1", b)], lambda e: e.scalar_tensor_tensor(
                out=xb, in0=xb, scalar=rstd[:, si:si + 1], in1=l1b[:], op0=ALU.mult, op1=ALU.add))

        def step_H(b):
            tc = b * 128

            def f(e):
                last = None
                for kc in range(8):
                    last = e.transpose(psT[:, kc * 128:(kc + 1) * 128], x1b[:, kc * 128:(kc + 1) * 128], ident[:])
                return last
            S.op("pe", ["x1b", "ident"], [("ps", 7)], f)
            def g(e):
                last = None
                for kc in range(8):
                    last = e.activation(out=x1T[:, kc, tc:tc + 128], in_=psT[:, kc * 128:(kc + 1) * 128],
                                        func=AF.Identity, scale=g1col[:, kc:kc + 1], bias=b1col[:, kc:kc + 1])
                return last
            S.op("act", [("ps", 7), "g1col", "b1col"], [("XT", b)], g)

        Wf1 = [arenaW[:, 0:4096].rearrange("p (k n) -> p k n", k=8), Wout[:, 0:4, :].rearrange("p a (b n) -> p (a b) n", b=2)]
        Wf2 = [arenaW[:, 4096:8192].rearrange("p (k n) -> p k n", k=4), Wout[:, 4:8, :]]
        hT = arenaW[:, 8192:14336].rearrange("p (s f t) -> p s f t", s=3, f=4)
        hf32 = uT[:, 0:2, :]
        pf = ps[3:7]
        pg = ps[0:3]
        wf1_v = wf1_d.rearrange("(kc p) n -> p kc n", p=128)
        wf2_v = wf2_d.rearrange("(fc p) n -> p fc n", p=128)

        def load_w(g):
            s = g % NSLOT
            S.dma("pool", Wf1[s], wf1_v[:, :, g * G:(g + 1) * G], [], [("Wf1", s)])
            S.dma("pool", Wf2[s], wf2_v[:, g * 4:(g + 1) * 4, :], [], [("Wf2", s)])

        cntf = [0]
        cntg = [0]

        def ff1(g, tg, hs):
            s = g % NSLOT
            if g == 0 and tg > 0:
                for bi_ in range(4):
                    ffn_init(tg * 4 + bi_)
            for fc in range(4):
                bank = cntf[0] % 4
                st = cntf[0] % 2
                cntf[0] += 1

                def f(e, fc=fc, bank=bank):
                    last = None
                    for kc in range(8):
                        last = e.matmul(pf[bank][:, :], Wf1[s][:, kc, fc * 128:(fc + 1) * 128],
                                        x1T[:, kc, tg * 512:(tg + 1) * 512], start=(kc == 0), stop=(kc == 7))
                    return last
                S.op("pe", [("Wf1", s)] + [("XT", 4 * tg + i) for i in range(4)], [("ps", 3 + bank)], f)
                S.op("act", [("ps", 3 + bank)], [("hf32", st)], lambda e, bank=bank, st=st: e.activation(
                    out=hf32[:, st, :], in_=pf[bank][:, :], func=AF.Relu))
                S.op("act", [("hf32", st)], [("hT", hs, fc)], lambda e, st=st, fc=fc: e.activation(
                    out=hT[:, hs, fc, :], in_=hf32[:, st, :], func=AF.Square))

        def ff2(g, tg, hs):
            s = g % NSLOT
            for bi in range(4):
                b = tg * 4 + bi
                xb = x1[:, b, :]
                for hf in range(2):
                    bank = cntg[0] % 3
                    cntg[0] += 1

                    def f(e, bank=bank, hf=hf, bi=bi):
                        last = None
                        for fc in range(4):
                            last = e.matmul(pg[bank][:, :], hT[:, hs, fc, bi * 128:(bi + 1) * 128],
                                            Wf2[s][:, fc, hf * 512:(hf + 1) * 512], start=(fc == 0), stop=(fc == 3))
                        return last
                    S.op("pe", [("Wf2", s)] + [("hT", hs, fc) for fc in range(4)], [("ps", bank)], f)
                    if g == 0:
                        S.op("dve", [("ps", bank), ("x1", b)], [("x1", b)], lambda e, bank=bank, hf=hf, xb=xb:
                             e.scalar_tensor_tensor(out=xb[:, hf * 512:(hf + 1) * 512],
                                                    in0=xb[:, hf * 512:(hf + 1) * 512], scalar=ALPHA,
                                                    in1=pg[bank][:, :], op0=ALU.mult, op1=ALU.add))
                    else:
                        S.op("dve", [("ps", bank), ("x1", b)], [("x1", b)], lambda e, bank=bank, hf=hf, xb=xb:
                             e.tensor_tensor(out=xb[:, hf * 512:(hf + 1) * 512], in0=pg[bank][:, :],
                                             in1=xb[:, hf * 512:(hf + 1) * 512], op=ALU.add))
                if g == NG - 1:
                    if len(pend) >= 2:
                        ln2_tail(pend.pop(0))
                    ln_stats(xb, ("x1", b), b % NSS)
                    pend.append(b)

        def ff2pair(tg, sa, sb):
            for bi in range(4):
                b = tg * 4 + bi
                xb = x1[:, b, :]
                for hf in range(2):
                    bank = cntg[0] % 3
                    cntg[0] += 1

                    def f(e, bank=bank, hf=hf, bi=bi):
                        last = None
                        n = 0
                        for ws, hs in ((0, sa), (1, sb)):
                            for fc in range(4):
                                last = e.matmul(pg[bank][:, :], hT[:, hs, fc, bi * 128:(bi + 1) * 128],
                                                Wf2[ws][:, fc, hf * 512:(hf + 1) * 512],
                                                start=(n == 0), stop=(n == 7))
                                n += 1
                        return last
                    S.op("pe", [("Wf2", 0), ("Wf2", 1)] + [("hT", sa, fc) for fc in range(4)]
                         + [("hT", sb, fc) for fc in range(4)], [("ps", bank)], f)
                    S.op("dve", [("ps", bank), ("x1", b)], [("x1", b)], lambda e, bank=bank, hf=hf, xb=xb:
                         e.tensor_tensor(out=xb[:, hf * 512:(hf + 1) * 512], in0=pg[bank][:, :],
                                         in1=xb[:, hf * 512:(hf + 1) * 512], op=ALU.add))
                ln_stats(xb, ("x1", b), b % NSS)
                pend.append(b)
                if len(pend) > 2:
                    ln2_tail(pend.pop(0))

        pend = []

        def ln2_tail(b):
            xb = x1[:, b, :]
            si = b % NSS
            if b % 2 == 1 or b >= NB - 3:
                S.op("dve", [("x1", b), ("mv", si), "l1g"], [("x1", b)], lambda e: e.scalar_tensor_tensor(
                    out=xb, in0=xb, scalar=mv[:, si, 0:1], in1=l1g[:], op0=ALU.subtract, op1=ALU.mult))
                S.op("dve", [("x1", b), ("rstd", si), "l1b"], [("x1", b)], lambda e: e.scalar_tensor_tensor(
                    out=xb, in0=xb, scalar=rstd[:, si:si + 1], in1=l1b[:], op0=ALU.mult, op1=ALU.add))
            else:
                ln_apply(xb, (l1g[:], "l1g"), (l1b[:], "l1b"), ("x1", b), "pool", "act", si)
            S.dma("sp", out_d[b * 128:(b + 1) * 128, :], xb, [("x1", b)], [])

        def reload_ln():
            S.dma("sp", l1g[:], l2g_d, [], ["l1g"])
            S.dma("sp", l1b[:], l2b_d, [], ["l1b"])

        seq = [(g, tg) for g in range(NG - 2) for tg in range(4)]
        p2ops = [lambda: ff1(seq[0][0], seq[0][1], 0)]
        for i, (g, tg) in enumerate(seq):
            if i + 1 < len(seq):
                p2ops.append(lambda i=i: ff1(seq[i + 1][0], seq[i + 1][1], (i + 1) % 2))
                if seq[i + 1] == (0, 3):
                    p2ops.append(reload_ln)
            else:
                p2ops.append(lambda: ff1(NG - 2, 0, 0))
            p2ops.append(lambda i=i, g=g, tg=tg: ff2(g, tg, i % 2))
            if tg == 3 and g + NSLOT < NG:
                p2ops.append(lambda g=g: load_w(g + NSLOT))
        GA, GB = NG - 2, NG - 1
        p2ops += [lambda: ff1(GB, 0, 1), lambda: ff1(GA, 1, 2), lambda: ff2pair(0, 0, 1),
                  lambda: ff1(GB, 1, 0), lambda: ff1(GA, 2, 1), lambda: ff2pair(1, 2, 0),
                  lambda: ff1(GB, 2, 2), lambda: ff1(GA, 3, 0), lambda: ff2pair(2, 1, 2),
                  lambda: ff1(GB, 3, 1), lambda: ff2pair(3, 0, 1)]
        p2ops.reverse()
        halo()
        for it in range(NB + 5):
            if 0 <= it - 2 < NB:
                step_E(it - 2)
            if 0 <= it - 4 < NB:
                G_tail(it - 4)
            if it >= 2 and it - 1 < NB:
                step_C(it - 1)
            if it == NB + 1:
                for bi_ in range(4):
                    ffn_init(bi_)
            if it < NB:
                A_pe(it)
            if it == 0:
                halo_b()
            if it == 1:
                step_C(0)
            if 0 <= it - 2 < NB and it < NB:
                F_pe(it - 2)
                F_evac(it - 2)
            if 0 <= it - 1 < NB:
                D_all(it - 1)
                D_exp(it - 1, 0)
                B_pe(it - 1)
                D_exp(it - 1, 1)
            if 0 <= it - 2 < NB and it >= NB:
                F_pe(it - 2)
                F_evac(it - 2)
            if it == 1:
                step_U(0)
            if it < NB:
                rope_k(it + 1, it % 2)
            if 0 <= it - 1 < NB:
                B_pool(it - 1)
            if it < NB:
                rope_q(it + 1, it % 2)
            if it < NB:
                A_vstats(it)
            if 0 <= it - 4 < NB:
                step_H(it - 4)
            if 0 <= it - 3 < NB:
                G_pe(it - 3)
            if it < NB:
                A_vtail(it)
            if 0 <= it - 4 < NB:
                G_late_act(it - 4)
            late_loads(it)
            if it % 4 == 0 and 0 < it < NB:
                step_U(it // 4)
            if it == NB - 1:
                S.alias([("Wf1", 0), ("Wf2", 0)] + [("hT", s_, f_) for s_ in range(3) for f_ in range(4)], WIN + WU)
                load_w(0)
            if it == NB:
                S.alias([("hf32", 0), ("hf32", 1)], [("uT", j) for j in range(4)])
                p2ops.pop()()
            if it == NB + 1:
                p2ops.pop()()
                p2ops.pop()()
            if it == NB + 2:
                S.alias([("Wf1", 1), ("Wf2", 1)], WOUT)
                load_w(1)
                p2ops.pop()()
                p2ops.pop()()
            if it == NB + 3:
                p2ops.pop()()
        while p2ops:
            p2ops.pop()()
        while pend:
            ln2_tail(pend.pop(0))
        S.drain("sp")
    return nc


_CACHE = {}


def kernel(x, positions, w_in, v_ln_g, v_ln_b, w_spatial, b_spatial, sinks, w_out,
           ln1_g, ln1_b, w_ff1, w_ff2, ln2_g, ln2_b):
    f32 = np.float32
    x = np.asarray(x, f32)
    positions = np.asarray(positions, np.int32)
    B, SEQ, _ = x.shape

    def rep(v, n):
        return np.ascontiguousarray(np.broadcast_to(np.asarray(v, f32).reshape(1, n), (128, n)))

    s_idx = np.arange(128)[:, None]
    q_idx = np.arange(128)[None, :]
    m_prev = (s_idx > q_idx).astype(f32)
    m_cur = (s_idx <= q_idx).astype(f32)
    inv_freq64 = 10000.0 ** (-np.arange(0, 64, 2, dtype=np.float64) / 64.0)
    inv_freq = inv_freq64.astype(f32)
    inv_freq_lo = (inv_freq64 - inv_freq.astype(np.float64)).astype(f32)
    common = {
        "w_in": np.ascontiguousarray(np.asarray(w_in, f32)[0]),
        "w_out": np.ascontiguousarray(np.asarray(w_out, f32)[0]),
        "w_ff1": np.ascontiguousarray(np.asarray(w_ff1, f32)[0]),
        "w_ff2": np.ascontiguousarray(np.asarray(w_ff2, f32)[0]),
        "w_spT": np.ascontiguousarray(np.asarray(w_spatial, f32)[0].transpose(2, 0, 1)),
        "bT": np.ascontiguousarray(np.broadcast_to(
            np.asarray(b_spatial, f32)[0].reshape(4, 2, 1, 128).transpose(1, 2, 0, 3), (2, 64, 4, 128)
        ).reshape(128, 4, 128)),
        "gcol": np.ascontiguousarray(np.asarray(v_ln_g, f32)[0].reshape(4, 2, 64).transpose(1, 2, 0).reshape(128, 4)),
        "bcol": np.ascontiguousarray(np.asarray(v_ln_b, f32)[0].reshape(4, 2, 64).transpose(1, 2, 0).reshape(128, 4)),
        "g1col": np.ascontiguousarray(np.asarray(ln1_g, f32)[0].reshape(8, 128).T),
        "b1col": np.ascontiguousarray(np.asarray(ln1_b, f32)[0].reshape(8, 128).T),
        "l1g": rep(np.asarray(ln1_g)[0], D),
        "l1b": rep(np.asarray(ln1_b)[0], D),
        "l2g": rep(np.asarray(ln2_g)[0], D),
        "l2b": rep(np.asarray(ln2_b)[0], D),
        "esk": rep(np.asarray(sinks)[0], 8),
        "ident": np.eye(128, dtype=f32),
        "invf": rep(inv_freq, 32),
        "invf_lo": rep(inv_freq_lo, 32),
    }
    in_maps = []
    for c in range(NCORES):
        b, h = divmod(c, 2)
        t0 = h * T
        xs = x[b, t0:t0 + T]
        xT = np.zeros((D, T + 128), f32)
        xT[:, 128:] = xs.T
        pos = np.zeros((NB + 1) * 128, np.int32)
        pos[128:] = positions[b, t0:t0 + T]
        msk = np.zeros((128, 2, 2, 128), f32)
        msk[:, 1, 0] = m_prev
        msk[:, 1, 1] = m_cur
        msk[:, 0, 1] = m_cur
        if h == 1:
            xT[:, :128] = x[b, t0 - 128:t0].T
            pos[:128] = positions[b, t0 - 128:t0]
            msk[:, 0, 0] = m_prev
        mb = np.where(msk > 0.5, f32(0.0), f32(-30000.0)).astype(f32)
        m = dict(common)
        m["mbias"] = np.ascontiguousarray(np.broadcast_to(mb[:, :, :, None, :], (128, 2, 2, 4, 128)).reshape(128, 2, 2, 512))
        m["x"] = np.ascontiguousarray(xs)
        m["xT"] = xT
        m["pos"] = np.ascontiguousarray(pos.reshape(NB + 1, 128).T)
        m["masks"] = msk
        in_maps.append(m)

    if "nc" not in _CACHE:
        _CACHE["nc"] = build_program()
    res = run_bass_kernel_spmd(_CACHE["nc"], in_maps, core_ids=list(range(NCORES)))
    out = np.empty((B, SEQ, D), f32)
    for c in range(NCORES):
        b, h = divmod(c, 2)
        out[b, h * T:(h + 1) * T] = res.results[c]["out"]
    return out
```

```python
import math
from contextlib import ExitStack

import numpy as np
import concourse.bass as bass
import concourse.mybir as mybir
from concourse.bass_utils import run_bass_kernel_spmd

F32 = mybir.dt.float32
BF16 = mybir.dt.bfloat16
I32 = mybir.dt.int32
AF = mybir.ActivationFunctionType
ALU = mybir.AluOpType

NCORES = 8
T = 2048
NB = T // 128
D = 1024
DIN = 1792
DFF = 4096
G = 512
NG = DFF // G
ALPHA = (2.0 * 1) ** 0.25
EPS = 1e-5
NSLOT = 2


class Sched:
    def __init__(self, nc, sems, dma_sems):
        self.nc = nc
        self.E = {"pe": nc.tensor, "act": nc.scalar, "dve": nc.vector, "pool": nc.gpsimd, "sp": nc.sync}
        self.sem = sems
        self.cnt = {k: 0 for k in sems}
        self.dsem = dma_sems
        self.dval = [0] * len(dma_sems)
        self.dnext = {"sp": 0, "pool": 0}
        self.waited = {e: {} for e in self.E}
        self.lastw = {}
        self.readers = {}
        self.extra = {}

    def _deps(self, reads, writes):
        need = {}

        def add(k, v):
            if need.get(k, 0) < v:
                need[k] = v

        for r in list(reads) + list(writes):
            for k, v in self.extra.get(r, {}).items():
                add(k, v)
        for r in reads:
            t = self.lastw.get(r)
            if t:
                add(*t)
        for w in writes:
            t = self.lastw.get(w)
            if t:
                add(*t)
            for k, v in self.readers.get(w, {}).items():
                add(k, v)
        return need

    def _wait(self, e, need):
        for k, v in need.items():
            if k == "pe" and e == "pe":
                continue
            if self.waited[e].get(k, 0) >= v:
                continue
            h = self.sem[k] if isinstance(k, str) else self.dsem[k]
            self.E[e].wait_ge(h, v)
            self.waited[e][k] = v

    def _commit(self, tok, reads, writes):
        k, v = tok
        for r in reads:
            d = self.readers.setdefault(r, {})
            if d.get(k, 0) < v:
                d[k] = v
        for w in writes:
            self.lastw[w] = tok
            self.readers[w] = {}

    def op(self, e, reads, writes, fn):
        self._wait(e, self._deps(reads, writes))
        ins = fn(self.E[e])
        self.cnt[e] += 1
        ins.then_inc(self.sem[e], 1)
        self._commit((e, self.cnt[e]), reads, writes)

    def dma(self, q, out, in_, reads, writes):
        half = len(self.dsem) // 2
        base = 0 if q == "sp" else half
        i = base + self.dnext[q] % half
        self.dnext[q] += 1
        need = self._deps(reads, writes)
        if self.dval[i] > 0 and need.get(i, 0) < self.dval[i]:
            need[i] = self.dval[i]
        self._wait(q, need)
        self.E[q].dma_start(out=out, in_=in_).then_inc(self.dsem[i], 16)
        self.dval[i] += 16
        self._commit((i, self.dval[i]), reads, writes)

    def alias(self, new_res, old_res):
        need = self._deps([], old_res)
        for r in new_res:
            d = self.extra.setdefault(r, {})
            for k, v in need.items():
                if d.get(k, 0) < v:
                    d[k] = v

    def drain(self, q):
        need = {i: v for i, v in enumerate(self.dval) if v > 0}
        self._wait(q, need)


def build_program():
    nc = bass.Bass("TRN2", target_bir_lowering=False)

    def din(name, shape, dt=F32):
        return nc.dram_tensor(name, shape, dt, kind="ExternalInput").ap()

    x_d = din("x", [T, D])
    xT_d = din("xT", [D, T + 128])
    pos_d = din("pos", [128, NB + 1], I32)
    win_d = din("w_in", [D, DIN])
    wout_d = din("w_out", [D, D])
    wf1_d = din("w_ff1", [D, DFF])
    wf2_d = din("w_ff2", [DFF, D])
    wsp_d = din("w_spT", [128, 8, 128])
    bT_d = din("bT", [128, 4, 128])
    l1g_d = din("l1g", [128, D])
    l1b_d = din("l1b", [128, D])
    l2g_d = din("l2g", [128, D])
    l2b_d = din("l2b", [128, D])
    esk_d = din("esk", [128, 8])
    idn_d = din("ident", [128, 128])
    msk_d = din("masks", [128, 2, 2, 128])
    invf_d = din("invf", [128, 32])
    invl_d = din("invf_lo", [128, 32])
    mb_d = din("mbias", [128, 2, 2, 512])
    gcol_d = din("gcol", [128, 4])
    g1c_d = din("g1col", [128, 8])
    b1c_d = din("b1col", [128, 8])
    bcol_d = din("bcol", [128, 4])
    out_d = nc.dram_tensor("out", [T, D], F32, kind="ExternalOutput").ap()

    es = ExitStack()
    with es:
        sems = {k: es.enter_context(nc.semaphore("sem_" + k)) for k in ("pe", "act", "dve", "pool")}
        dsems = [es.enter_context(nc.semaphore("semd%d" % i)) for i in range(16)]
        S = Sched(nc, sems, dsems)

        def sb(name, shape, dt):
            return es.enter_context(nc.sbuf_tensor("sb_" + name, shape, dt))

        def pst(name, shape, dt):
            return es.enter_context(nc.psum_tensor("pp_" + name, shape, dt))

        x1 = sb("x1", [128, NB, D], F32)
        x1T = sb("x1T", [128, 8, T], BF16)
        NSS = NB
        st6 = sb("st6", [128, NSS, 2, 6], F32)
        mv = sb("mv", [128, NSS, 2], F32)
        ve = sb("ve", [128, NSS], F32)
        rstd = sb("rstd", [128, NSS], F32)
        mhalf = sb("mhalf", [128, 1], F32)
        nmr = sb("nmr", [128, NSS], F32)

        arenaW = sb("arenaW", [128, 8 * DIN], BF16)
        Win = arenaW[:, :].rearrange("p (k n) -> p k n", k=8)
        Wout = sb("Wout", [128, 8, D], BF16)
        xTh = sb("xTh", [128, 8, 128], BF16)
        uT = sb("uT", [128, 4, 512], F32)
        ident = sb("ident", [128, 128], BF16)
        masks = sb("masks", [128, 2, 2, 128], BF16)
        invf = sb("invf", [128, 32], F32)
        invl = sb("invl", [128, 32], F32)
        posi = sb("posi", [128, NB + 1], I32)
        posf = sb("posf", [128, NB + 1], F32)
        Ct = sb("Ct", [128, NB + 1, 32], F32)
        St = sb("St", [128, NB + 1, 32], F32)
        WcT = sb("WcT", [128, 8, 128], BF16)
        bT = sb("bT", [128, 4, 128], F32)
        mbias = sb("mbias", [128, 2, 2, 512], BF16)
        gcol = sb("gcol", [128, 4], F32)
        g1col = sb("g1col", [128, 8], F32)
        b1col = sb("b1col", [128, 8], F32)
        bcol = sb("bcol", [128, 4], F32)
        ones = sb("ones", [128, 128], BF16)
        st6v = sb("st6v", [128, 1, 6], F32)
        mvv = sb("mvv", [128, 2], F32)
        vev = sb("vev", [128, 1], F32)
        rstdv = sb("rstdv", [128, 1], F32)
        l1g = sb("l1g", [128, D], F32)
        l1b = sb("l1b", [128, D], F32)
        esk = sb("esk", [128, 8], F32)
        esink = sb("esink", [128, 8], F32)
        vg = sb("vg", [128, 512], F32)
        vln = sb("vln", [128, 2, 512], BF16)
        qk = sb("qk", [128, 640], F32)
        rt = sb("rt", [128, 3, 2, 128], F32)
        qkr = sb("qkr", [128, 2, 640], BF16)
        qT = sb("qT", [128, 512], BF16)
        kT = sb("kT", [128, 2, 128], BF16)
        vext = sb("vext", [128, 4, 2, 65], BF16)
        PT = sb("PT", [128, 4, 512], BF16)
        den = sb("den", [128, 8], F32)
        rden = sb("rden", [128, 8], F32)
        oatt = sb("oatt", [128, 512], BF16)
        catT = sb("catT", [128, 3, 8, 128], BF16)
        tmpa = sb("tmpa", [128, 512], F32)
        x1b = sb("x1b", [128, D], BF16)
        ps = [pst("ps%d" % i, [128, 512], F32) for i in range(7)]
        psT = pst("psT", [128, 1024], BF16)

        NP = (NB + 1) * 32
        uflat = uT[:].rearrange("p j t -> p (j t)")
        ang = uflat[:, 0:NP].rearrange("p (b d) -> p b d", b=NB + 1)
        kf = uflat[:, NP:2 * NP].rearrange("p (b d) -> p b d", b=NB + 1)
        mk = uflat[:, 2 * NP:3 * NP].rearrange("p (b d) -> p b d", b=NB + 1)
        r2 = qk[:, 0:NP].rearrange("p (b d) -> p b d", b=NB + 1)
        wspb = x1b[:].rearrange("p (h t) -> p h t", h=8)
        ANG = [("uT", 0), ("uT", 1)]
        KF = [("uT", 1), ("uT", 2)]
        MK = [("uT", 2), ("uT", 3)]
        R2 = ["qk_q", "qk_k"]

        def ln_stats(buf, res, si):
            def f(e):
                last = None
                for h in range(2):
                    last = e.bn_stats(st6[:, si, h, :], buf[:, h * 512:(h + 1) * 512])
                return last
            S.op("dve", [res], [("st6", si)], f)
            S.op("dve", [("st6", si)], [("mv", si)], lambda e: e.bn_aggr(mv[:, si, :], st6[:, si, :, :]))
            S.op("dve", [("mv", si)], [("ve", si)], lambda e: e.tensor_scalar(
                out=ve[:, si:si + 1], in0=mv[:, si, 1:2], scalar1=EPS, scalar2=None, op0=ALU.add))
            S.op("pool", [("ve", si), "mhalf"], [("rstd", si)], lambda e: e.tensor_tensor(
                out=rstd[:, si:si + 1], in0=ve[:, si:si + 1], in1=mhalf[:], op=ALU.pow))

        def ln_apply(buf, gt, bt, res, g_eng, n_eng, si):
            if n_eng == "dve":
                S.op("dve", [res, ("mv", si), ("rstd", si)], [res], lambda e: e.tensor_scalar(
                    out=buf, in0=buf, scalar1=mv[:, si, 0:1], scalar2=rstd[:, si:si + 1],
                    op0=ALU.subtract, op1=ALU.mult))
            else:
                S.op("dve", [("mv", si), ("rstd", si)], [("nmr", si)], lambda e: e.scalar_tensor_tensor(
                    out=nmr[:, si:si + 1], in0=mv[:, si, 0:1], scalar=-1.0, in1=rstd[:, si:si + 1],
                    op0=ALU.mult, op1=ALU.mult))
                S.op("act", [res, ("nmr", si), ("rstd", si)], [res], lambda e: e.activation(
                    out=buf, in_=buf, func=AF.Identity, bias=nmr[:, si:si + 1], scale=rstd[:, si:si + 1]))
            S.op(g_eng, [res, gt[1]], [res], lambda e: e.tensor_tensor(
                out=buf, in0=buf, in1=gt[0], op=ALU.mult))
            S.op("pool", [res, bt[1]], [res], lambda e: e.tensor_tensor(
                out=buf, in0=buf, in1=bt[0], op=ALU.add))

        S.op("dve", [], ["mhalf"], lambda e: e.memset(mhalf[:], -0.5))

        xT_v = xT_d.rearrange("(kc p) t -> p kc t", p=128)
        win_v = win_d.rearrange("(kc p) n -> p kc n", p=128)
        wout_v = wout_d.rearrange("(kc p) n -> p kc n", p=128)
        S.dma("pool", masks[:], msk_d, [], ["masks"])
        S.dma("pool", wspb, wsp_d, [], ["x1b"])
        S.dma("pool", xTh[:], xT_v[:, :, 0:128], [], ["xTh"])
        S.dma("pool", x1T[:, :, 0:128], xT_v[:, :, 128:256], [], [("XT", 0)])
        for kc in range(8):
            S.dma("pool", Win[:, kc, 512:DIN], win_v[:, kc, 512:DIN], [], [("Win", kc)])
        S.dma("pool", ident[:], idn_d, [], ["ident"])
        S.dma("pool", x1T[:, :, 128:256], xT_v[:, :, 256:384], [], [("XT", 1)])
        S.dma("sp", posi[:], pos_d, [], ["posi"])
        S.dma("sp", invf[:], invf_d, [], ["invf"])
        S.dma("sp", invl[:], invl_d, [], ["invl"])
        S.dma("sp", esk[:], esk_d, [], ["esk"])
        S.dma("sp", gcol[:], gcol_d, [], ["gcol"])
        S.dma("sp", g1col[:], g1c_d, [], ["g1col"])
        S.dma("sp", b1col[:], b1c_d, [], ["b1col"])
        S.dma("sp", bcol[:], bcol_d, [], ["bcol"])
        S.dma("sp", bT[:], bT_d, [("Win", 7)], ["bT"])

        def x_load(b):
            S.dma("sp", x1[:, b, :], x_d[b * 128:(b + 1) * 128, :], [("Win", 7), "tick"], [("x1", b)])

        S.dma("pool", mbias[:], mb_d, [("Win", 7)], ["mbias"])
        S.dma("pool", x1T[:, :, 256:512], xT_v[:, :, 384:128 + 512], [], [("XT", 2), ("XT", 3)])
        for kc in range(8):
            S.dma("pool", Win[:, kc, 0:512], win_v[:, kc, 0:512], [], [("Wu", kc)])

        def late_loads(it):
            if it == 0:
                for kc in range(8):
                    S.dma("pool", Wout[:, kc, :], wout_v[:, kc, :], [], [("Wout", kc)])
                S.dma("sp", l1g[:], l1g_d, ["tick"], ["l1g"])
                S.dma("sp", l1b[:], l1b_d, ["tick"], ["l1b"])
            if it == 1:
                for gi in range(1, 4):
                    S.dma("pool", x1T[:, :, gi * 512:(gi + 1) * 512],
                          xT_v[:, :, 128 + gi * 512:128 + (gi + 1) * 512], [], [("XT", 4 * gi + j) for j in range(4)])
            if it + 2 < NB:
                x_load(it + 2)

        x_load(0)
        x_load(1)
        WIN = [("Win", kc) for kc in range(8)]
        WU = [("Wu", kc) for kc in range(8)]
        WOUT = [("Wout", kc) for kc in range(8)]

        TWO_PI = 2.0 * math.pi
        C1 = 6.28125
        C2 = TWO_PI - C1
        MAGIC = 12582912.0
        PI_SAFE = 3.141592
        S.op("dve", ["posi"], ["posf"], lambda e: e.tensor_copy(posf[:], posi[:]))
        S.op("dve", ["posf", "invf"], ANG, lambda e: e.tensor_tensor(
            out=ang, in0=invf[:].unsqueeze(1).broadcast_to([128, NB + 1, 32]),
            in1=posf[:].unsqueeze(2).broadcast_to([128, NB + 1, 32]), op=ALU.mult))

        def fold(buf, tags):
            S.op("dve", tags, MK, lambda e: e.tensor_single_scalar(
                out=mk, in_=buf, scalar=math.pi, op=ALU.is_gt))
            S.op("dve", tags + MK, tags, lambda e: e.scalar_tensor_tensor(
                out=buf, in0=mk, scalar=-TWO_PI, in1=buf, op0=ALU.mult, op1=ALU.add))
            S.op("dve", tags, MK, lambda e: e.tensor_single_scalar(
                out=mk, in_=buf, scalar=-math.pi, op=ALU.is_lt))
            S.op("dve", tags + MK, tags, lambda e: e.scalar_tensor_tensor(
                out=buf, in0=mk, scalar=TWO_PI, in1=buf, op0=ALU.mult, op1=ALU.add))
            S.op("dve", tags, tags, lambda e: e.tensor_scalar(
                out=buf, in0=buf, scalar1=PI_SAFE, scalar2=-PI_SAFE, op0=ALU.min, op1=ALU.max))

        S.op("dve", ANG, KF, lambda e: e.tensor_scalar(
            out=kf, in0=ang, scalar1=1.0 / TWO_PI, scalar2=MAGIC, op0=ALU.mult, op1=ALU.add))
        S.op("dve", KF, KF, lambda e: e.tensor_scalar(
            out=kf, in0=kf, scalar1=-MAGIC, scalar2=None, op0=ALU.add))
        S.op("dve", ANG + KF, ANG, lambda e: e.scalar_tensor_tensor(
            out=ang, in0=kf, scalar=-C1, in1=ang, op0=ALU.mult, op1=ALU.add))
        S.op("dve", ANG + KF, ANG, lambda e: e.scalar_tensor_tensor(
            out=ang, in0=kf, scalar=-C2, in1=ang, op0=ALU.mult, op1=ALU.add))
        S.op("dve", ["posf", "invl"], MK, lambda e: e.tensor_tensor(
            out=mk, in0=invl[:].unsqueeze(1).broadcast_to([128, NB + 1, 32]),
            in1=posf[:].unsqueeze(2).broadcast_to([128, NB + 1, 32]), op=ALU.mult))
        S.op("dve", ANG + MK, ANG, lambda e: e.tensor_tensor(out=ang, in0=ang, in1=mk, op=ALU.add))
        S.op("dve", ANG, R2, lambda e: e.tensor_scalar(
            out=r2, in0=ang, scalar1=0.5 * math.pi, scalar2=None, op0=ALU.add))
        fold(ang, ANG)
        fold(r2, R2)
        S.op("act", ANG, ["St"], lambda e: e.activation(out=St[:], in_=ang, func=AF.Sin))
        S.op("act", R2, ["Ct"], lambda e: e.activation(out=Ct[:], in_=r2, func=AF.Sin))
        S.op("act", ["esk"], ["esink"], lambda e: e.activation(out=esink[:], in_=esk[:], func=AF.Exp))
        S.op("dve", ["x1b", "masks"], ["WcT"], lambda e: e.tensor_tensor(
            out=WcT[:], in0=wspb, in1=masks[:, 1, 1, :].unsqueeze(1).broadcast_to([128, 8, 128]),
            op=ALU.mult))
        S.op("dve", [], [("vext", i) for i in range(4)], lambda e: e.memset(vext[:], 1.0))

        S.op("dve", [], ["ones"], lambda e: e.memset(ones[:], 1.0))

        csT = tmpa[:].rearrange("p (j t) -> p j t", j=4)
        for half in range(2):
            S.op("pe", ["WcT", "ones"], [("ps", half)], lambda e, half=half: e.matmul(
                ps[half][:, :], ones[:], WcT[:, 4 * half:4 * half + 4, :].rearrange("p h t -> p (h t)"),
                start=True, stop=True))
            pv = ps[half][:, :].rearrange("p (j hp t) -> p j hp t", j=2, hp=2)
            for hp in range(2):
                S.op("dve", [("ps", half)], ["tmpa"], lambda e, half=half, hp=hp, pv=pv: e.tensor_copy(
                    csT[hp * 64:(hp + 1) * 64, 2 * half:2 * half + 2, :], pv[hp * 64:(hp + 1) * 64, :, hp, :]))
        S.op("dve", ["tmpa", "bcol"], ["tmpa"], lambda e: e.tensor_tensor(
            out=csT, in0=csT, in1=bcol[:].unsqueeze(2).broadcast_to([128, 4, 128]), op=ALU.mult))
        S.op("dve", ["tmpa", "bT"], ["bT"], lambda e: e.tensor_tensor(out=bT[:], in0=bT[:], in1=csT, op=ALU.add))

        def rope_ops(eng, bidx, src1, src2, dst1, dst2, shape, rd, wr, ti):
            cb = Ct[:, bidx, :].unsqueeze(1).broadcast_to(shape)
            sbv = St[:, bidx, :].unsqueeze(1).broadcast_to(shape)
            n = shape[1] * shape[2]
            ta = rt[:, ti, 0, 0:n].rearrange("p (a d) -> p a d", a=shape[1])
            tb = rt[:, ti, 1, 0:n].rearrange("p (a d) -> p a d", a=shape[1])
            r0, r1 = ("rt0", ti), ("rt1", ti)
            S.op(eng, rd + ["Ct"], [r0], lambda e: e.tensor_tensor(out=ta, in0=src1, in1=cb, op=ALU.mult))
            S.op(eng, rd + ["St"], [r1], lambda e: e.tensor_tensor(out=tb, in0=src2, in1=sbv, op=ALU.mult))
            S.op(eng, [r0, r1], wr, lambda e: e.tensor_tensor(out=dst1, in0=ta, in1=tb, op=ALU.subtract))
            S.op(eng, rd + ["Ct"], [r0], lambda e: e.tensor_tensor(out=ta, in0=src2, in1=cb, op=ALU.mult))
            S.op(eng, rd + ["St"], [r1], lambda e: e.tensor_tensor(out=tb, in0=src1, in1=sbv, op=ALU.mult))
            S.op(eng, [r0, r1], wr, lambda e: e.tensor_tensor(out=dst2, in0=ta, in1=tb, op=ALU.add))

        def rope_q(bidx, qs):
            qs4 = qk[:, 0:512].rearrange("p (kv c two d) -> p kv c two d", kv=2, c=4, two=2)
            qd4 = qkr[:, qs, 0:512].rearrange("p (c kv two d) -> p kv c two d", kv=2, c=4, two=2)
            shape = [128, 2, 4, 32]
            cb = Ct[:, bidx, :].unsqueeze(1).unsqueeze(1).broadcast_to(shape)
            sbv = St[:, bidx, :].unsqueeze(1).unsqueeze(1).broadcast_to(shape)
            ta = rt[:, 0:2, 0, :].rearrange("p a (c d) -> p a c d", c=4)
            tb = rt[:, 0:2, 1, :].rearrange("p a (c d) -> p a c d", c=4)
            s1, s2 = qs4[:, :, :, 0, :], qs4[:, :, :, 1, :]
            d1, d2 = qd4[:, :, :, 0, :], qd4[:, :, :, 1, :]
            wr = [("qkr_q", qs, 0), ("qkr_q", qs, 1)]
            r0 = [("rt0", 0), ("rt0", 1)]
            r1 = [("rt1", 0), ("rt1", 1)]
            S.op("dve", ["qk_q", "Ct"], r0, lambda e: e.tensor_tensor(out=ta, in0=s1, in1=cb, op=ALU.mult))
            S.op("dve", ["qk_q", "St"], r1, lambda e: e.tensor_tensor(out=tb, in0=s2, in1=sbv, op=ALU.mult))
            S.op("dve", r0 + r1, wr, lambda e: e.tensor_tensor(out=d1, in0=ta, in1=tb, op=ALU.subtract))
            S.op("dve", ["qk_q", "Ct"], r0, lambda e: e.tensor_tensor(out=ta, in0=s2, in1=cb, op=ALU.mult))
            S.op("dve", ["qk_q", "St"], r1, lambda e: e.tensor_tensor(out=tb, in0=s1, in1=sbv, op=ALU.mult))
            S.op("dve", r0 + r1, wr, lambda e: e.tensor_tensor(out=d2, in0=ta, in1=tb, op=ALU.add))

        def rope_k(bidx, qs):
            qsrc = qk[:, 512:640].rearrange("p (h two d) -> p h two d", h=2, two=2)
            kdst = qkr[:, qs, 512:640].rearrange("p (h two d) -> p h two d", h=2, two=2)
            rope_ops("dve" if bidx <= 2 else "pool", bidx, qsrc[:, :, 0, :], qsrc[:, :, 1, :], kdst[:, :, 0, :], kdst[:, :, 1, :],
                     [128, 2, 32], ["qk_k"], [("qkr_k", qs)], 2)

        def halo():
            def f(e):
                last = None
                for kc in range(8):
                    last = e.matmul(ps[2][:, 0:256], xTh[:, kc, :], Win[:, kc, 1536:1792],
                                    start=(kc == 0), stop=(kc == 7))
                return last
            S.op("pe", WIN + ["xTh"], [("ps", 2)], f)
            S.op("act", [("ps", 2)], ["qk_k"], lambda e: e.copy(out=qk[:, 512:640], in_=ps[2][:, 0:128]))
            S.op("act", [("ps", 2)], [("vext", 3)], lambda e: e.copy(
                out=vext[:, 3, :, 0:64], in_=ps[2][:, 128:256].rearrange("p (h d) -> p h d", h=2)))
            rope_k(0, 1)

        def halo_b():
            S.op("pe", [("qkr_k", 1), "ident"], [("ps", 7)], lambda e: e.transpose(
                psT[:, 512:640], qkr[:, 1, 512:640], ident[:]))
            S.op("act", [("ps", 7)], [("kT", 1)], lambda e: e.copy(out=kT[:, 1, :], in_=psT[:, 512:640]))

        def step_U(gi):
            for j in range(4):
                def f(e, j=j):
                    last = None
                    for kc in range(8):
                        last = e.matmul(ps[3 + j][:, :], Win[:, kc, j * 128:(j + 1) * 128],
                                        x1T[:, kc, gi * 512:(gi + 1) * 512], start=(kc == 0), stop=(kc == 7))
                    return last
                S.op("pe", WU + [("XT", 4 * gi + i) for i in range(4)], [("ps", 3 + j)], f)
                S.op("act", [("ps", 3 + j)], [("uT", j)], lambda e, j=j: e.activation(
                    out=uT[:, j, :], in_=ps[3 + j][:, :], func=AF.Gelu_apprx_tanh))

        def A_pe(b):
            tc = b * 128
            vs = b % 4

            def f(e):
                last = None
                for kc in range(8):
                    l = x1T[:, kc, tc:tc + 128]
                    e.matmul(ps[0][:, :], l, Win[:, kc, 512:1024], start=(kc == 0), stop=(kc == 7))
                    e.matmul(ps[1][:, :], l, Win[:, kc, 1024:1536], start=(kc == 0), stop=(kc == 7))
                    last = e.matmul(ps[2][:, 0:256], l, Win[:, kc, 1536:1792], start=(kc == 0), stop=(kc == 7))
                return last
            S.op("pe", WIN + [("XT", b)], [("ps", 0), ("ps", 1), ("ps", 2)], f)
            S.op("dve", [("ps", 2)], ["qk_k"], lambda e: e.tensor_copy(qk[:, 512:640], ps[2][:, 0:128]))
            S.op("dve", [("ps", 2)], [("vext", vs)], lambda e: e.tensor_copy(
                vext[:, vs, :, 0:64], ps[2][:, 128:256].rearrange("p (h d) -> p h d", h=2)))
            S.op("dve", [("ps", 1)], ["qk_q"], lambda e: e.tensor_copy(qk[:, 0:512], ps[1][:, :]))
            S.op("act", [("ps", 0)], ["vg"], lambda e: e.activation(
                out=vg[:], in_=ps[0][:, :], func=AF.Gelu_apprx_tanh))

        def A_vstats(b):
            S.op("dve", ["vg"], ["st6v"], lambda e: e.bn_stats(st6v[:, 0, :], vg[:]))
            S.op("dve", ["st6v"], ["mvv"], lambda e: e.bn_aggr(mvv[:], st6v[:, 0:1, :]))
            S.op("dve", ["mvv"], ["vev"], lambda e: e.tensor_scalar(
                out=vev[:], in0=mvv[:, 1:2], scalar1=EPS, scalar2=None, op0=ALU.add))
            S.op("pool", ["vev", "mhalf"], ["rstdv", "tick"], lambda e: e.tensor_tensor(
                out=rstdv[:], in0=vev[:], in1=mhalf[:], op=ALU.pow))

        def A_vtail(b):
            s2 = b % 2
            S.op("dve", ["vg", "mvv", "rstdv"], [("vln", s2)], lambda e: e.tensor_scalar(
                out=vln[:, s2, :], in0=vg[:], scalar1=mvv[:, 0:1], scalar2=rstdv[:, 0:1],
                op0=ALU.subtract, op1=ALU.mult))

        def B_pe(b):
            bi = b % 4
            s2 = b % 2

            def f(e):
                last = None
                for h in range(8):
                    j, hp = divmod(h, 2)
                    last = e.matmul(ps[2][hp * 64:(hp + 1) * 64, j * 128:(j + 1) * 128],
                                    vln[:, s2, h * 64:(h + 1) * 64], WcT[:, h, :], start=True, stop=True)
                return last
            S.op("pe", [("vln", s2), "WcT"], [("ps", 2)], f)
            for j in range(4):
                S.op("act", [("ps", 2), "gcol"], [("tmpa", j)], lambda e, j=j: e.activation(
                    out=tmpa[:, j * 128:(j + 1) * 128], in_=ps[2][:, j * 128:(j + 1) * 128],
                    func=AF.Identity, scale=gcol[:, j:j + 1]))

        def B_pool(b):
            bi = b % 4
            s2 = b % 3
            TA = [("tmpa", j) for j in range(4)]
            S.op("pool", TA + ["bT"], TA, lambda e: e.tensor_tensor(
                out=tmpa[:], in0=tmpa[:], in1=bT[:].rearrange("p j t -> p (j t)"), op=ALU.add))
            S.op("pool", TA + [("uT", j) for j in range(4)], [("catT_a", s2)], lambda e: e.tensor_tensor(
                out=catT[:, s2, 0:4, :], in0=tmpa[:].rearrange("p (j t) -> p j t", j=4),
                in1=uT[:, :, bi * 128:(bi + 1) * 128], op=ALU.mult))

        def step_C(b):
            s2 = b % 2

            def f(e):
                last = None
                for j in range(5):
                    last = e.transpose(psT[:, j * 128:(j + 1) * 128], qkr[:, s2, j * 128:(j + 1) * 128], ident[:])
                return last
            S.op("pe", [("qkr_q", s2, 0), ("qkr_q", s2, 1), ("qkr_k", s2), "ident"], [("ps", 7)], f)
            S.op("act", [("ps", 7)], ["qT"], lambda e: e.copy(out=qT[:], in_=psT[:, 0:512]))
            S.op("act", [("ps", 7)], [("kT", s2)], lambda e: e.copy(out=kT[:, s2, :], in_=psT[:, 512:640]))

        def D_half(b, kv):
            cur = b % 2
            prv = 1 - cur
            mi = 0 if b == 0 else 1

            def f(e):
                last = None
                for kb, slot in ((0, prv), (1, cur)):
                    o = ps[3 + kv * 2 + kb][:, :]
                    e.matmul(o, kT[kv * 64:(kv + 1) * 64, slot, :], qT[kv * 64:(kv + 1) * 64, :],
                             start=True, stop=False)
                    last = e.matmul(o, ident[:], mbias[:, mi, kb, :], start=False, stop=True)
                return last
            S.op("pe", ["qT", ("kT", 0), ("kT", 1), "ident", "mbias"],
                 [("ps", 3 + kv * 2), ("ps", 4 + kv * 2)], f)

        def D_all(b):
            cur = b % 2
            prv = 1 - cur
            mi = 0 if b == 0 else 1

            def f(e):
                last = None
                for kb, slot in ((0, prv), (1, cur)):
                    for kv in range(2):
                        e.matmul(ps[3 + kv * 2 + kb][:, :], kT[kv * 64:(kv + 1) * 64, slot, :],
                                 qT[kv * 64:(kv + 1) * 64, :], start=True, stop=False)
                for kv in range(2):
                    for kb in range(2):
                        last = e.matmul(ps[3 + kv * 2 + kb][:, :], ident[:], mbias[:, mi, kb, :],
                                        start=False, stop=True)
                return last
            S.op("pe", ["qT", ("kT", 0), ("kT", 1), "ident", "mbias"],
                 [("ps", 3), ("ps", 4), ("ps", 5), ("ps", 6)], f)

        def D_exp(b, kv):
            for kb in range(2):
                jj = kv * 2 + kb
                S.op("act", [("ps", 3 + jj)], [("PT", jj)], lambda e, jj=jj: e.activation(
                    out=PT[:, jj, :], in_=ps[3 + jj][:, :], func=AF.Exp, scale=0.125))

        def step_E(b):
            cur = b % 4
            prv = (b - 1) % 4

            def f(e):
                last = None
                for kv in range(2):
                    o = ps[5 + kv][:, 0:260].rearrange("p (c d) -> p c d", c=4)
                    for c in range(4):
                        for kb, slot in ((0, prv), (1, cur)):
                            last = e.matmul(o[:, c, :], PT[:, kv * 2 + kb, c * 128:(c + 1) * 128],
                                            vext[:, slot, kv, :], start=(kb == 0), stop=(kb == 1))
                return last
            S.op("pe", [("PT", j) for j in range(4)] + [("vext", prv), ("vext", cur)], [("ps", 5), ("ps", 6)], f)
            for kv in range(2):
                o = ps[5 + kv][:, 0:260].rearrange("p (c d) -> p c d", c=4)
                S.op("dve", [("ps", 5 + kv), "esink"], ["den"], lambda e, kv=kv, o=o: e.tensor_tensor(
                    out=den[:, kv * 4:(kv + 1) * 4], in0=o[:, :, 64], in1=esink[:, kv * 4:(kv + 1) * 4],
                    op=ALU.add))
            S.op("dve", ["den"], ["rden"], lambda e: e.reciprocal(out=rden[:], in_=den[:]))
            for kv in range(2):
                o = ps[5 + kv][:, 0:260].rearrange("p (c d) -> p c d", c=4)
                S.op("dve", [("ps", 5 + kv), "rden"], ["oatt"], lambda e, kv=kv, o=o: e.tensor_tensor(
                    out=oatt[:, kv * 256:(kv + 1) * 256].rearrange("p (c d) -> p c d", c=4),
                    in0=o[:, :, 0:64],
                    in1=rden[:, kv * 4:(kv + 1) * 4].unsqueeze(2).broadcast_to([128, 4, 64]), op=ALU.mult))

        def F_pe(b):
            s2 = b % 2

            def f(e):
                last = None
                for j in range(4):
                    last = e.transpose(psT[:, j * 128:(j + 1) * 128], oatt[:, j * 128:(j + 1) * 128], ident[:])
                return last
            S.op("pe", ["oatt", "ident"], [("ps", 7)], f)

        def F_evac(b):
            s2 = b % 3
            S.op("act", [("ps", 7)], [("catT_b", s2)], lambda e: e.copy(
                out=catT[:, s2, 4:8, :], in_=psT[:, 0:512].rearrange("p (j t) -> p j t", j=4)))

        def G_pe(b):
            s2 = b % 3

            def f(e):
                last = None
                for kc in range(8):
                    for hf in range(2):
                        last = e.matmul(ps[hf][:, :], catT[:, s2, kc, :], Wout[:, kc, hf * 512:(hf + 1) * 512],
                                        start=(kc == 0), stop=(kc == 7))
                return last
            S.op("pe", WOUT + [("catT_a", s2), ("catT_b", s2)], [("ps", 0), ("ps", 1)], f)
            xb = x1[:, b, :]
            for hf in range(2):
                S.op("dve", [("ps", hf), ("x1", b)], [("x1", b)], lambda e, hf=hf: e.scalar_tensor_tensor(
                    out=xb[:, hf * 512:(hf + 1) * 512], in0=xb[:, hf * 512:(hf + 1) * 512], scalar=ALPHA,
                    in1=ps[hf][:, :], op0=ALU.mult, op1=ALU.add))
            ln_stats(xb, ("x1", b), b % NSS)

        def G_tail(b):
            xb = x1[:, b, :]
            si = b % NSS
            S.op("dve", [("x1", b), ("mv", si), ("rstd", si)], ["x1b"], lambda e: e.tensor_scalar(
                out=x1b[:], in0=xb, scalar1=mv[:, si, 0:1], scalar2=rstd[:, si:si + 1],
                op0=ALU.subtract, op1=ALU.mult))

        def G_late_act(b):
            xb = x1[:, b, :]
            si = b % NSS
            S.op("dve", [("x1", b), ("mv", si), "l1g"], [("x1", b)], lambda e: e.scalar_tensor_tensor(
                out=xb, in0=xb, scalar=mv[:, si, 0:1], in1=l1g[:], op0=ALU.subtract, op1=ALU.mult))

        def ffn_init(b):
            xb = x1[:, b, :]
            si = b % NSS
            S.op("dve", [("x1", b), ("rstd", si), "l1b"], [("x1", b)], lambda e: e.scalar_tensor_tensor(
                out=xb, in0=xb, scalar=rstd[:, si:si + 1], in1=l1b[:], op0=ALU.mult, op1=ALU.add))

        def step_H(b):
            tc = b * 128

            def f(e):
                last = None
                for kc in range(8):
                    last = e.transpose(psT[:, kc * 128:(kc + 1) * 128], x1b[:, kc * 128:(kc + 1) * 128], ident[:])
                return last
            S.op("pe", ["x1b", "ident"], [("ps", 7)], f)
            def g(e):
                last = None
                for kc in range(8):
                    last = e.activation(out=x1T[:, kc, tc:tc + 128], in_=psT[:, kc * 128:(kc + 1) * 128],
                                        func=AF.Identity, scale=g1col[:, kc:kc + 1], bias=b1col[:, kc:kc + 1])
                return last
            S.op("act", [("ps", 7), "g1col", "b1col"], [("XT", b)], g)

        Wf1 = [arenaW[:, 0:4096].rearrange("p (k n) -> p k n", k=8), Wout[:, 0:4, :].rearrange("p a (b n) -> p (a b) n", b=2)]
        Wf2 = [arenaW[:, 4096:8192].rearrange("p (k n) -> p k n", k=4), Wout[:, 4:8, :]]
        hT = arenaW[:, 8192:14336].rearrange("p (s f t) -> p s f t", s=3, f=4)
        hf32 = uT[:, 0:2, :]
        pf = ps[3:7]
        pg = ps[0:3]
        wf1_v = wf1_d.rearrange("(kc p) n -> p kc n", p=128)
        wf2_v = wf2_d.rearrange("(fc p) n -> p fc n", p=128)

        def load_w(g):
            s = g % NSLOT
            S.dma("pool", Wf1[s], wf1_v[:, :, g * G:(g + 1) * G], [], [("Wf1", s)])
            S.dma("pool", Wf2[s], wf2_v[:, g * 4:(g + 1) * 4, :], [], [("Wf2", s)])

        cntf = [0]
        cntg = [0]

        def ff1(g, tg, hs):
            s = g % NSLOT
            if g == 0 and tg > 0:
                for bi_ in range(4):
                    ffn_init(tg * 4 + bi_)
            for fc in range(4):
                bank = cntf[0] % 4
                st = cntf[0] % 2
                cntf[0] += 1

                def f(e, fc=fc, bank=bank):
                    last = None
                    for kc in range(8):
                        last = e.matmul(pf[bank][:, :], Wf1[s][:, kc, fc * 128:(fc + 1) * 128],
                                        x1T[:, kc, tg * 512:(tg + 1) * 512], start=(kc == 0), stop=(kc == 7))
                    return last
                S.op("pe", [("Wf1", s)] + [("XT", 4 * tg + i) for i in range(4)], [("ps", 3 + bank)], f)
                S.op("act", [("ps", 3 + bank)], [("hf32", st)], lambda e, bank=bank, st=st: e.activation(
                    out=hf32[:, st, :], in_=pf[bank][:, :], func=AF.Relu))
                S.op("act", [("hf32", st)], [("hT", hs, fc)], lambda e, st=st, fc=fc: e.activation(
                    out=hT[:, hs, fc, :], in_=hf32[:, st, :], func=AF.Square))

        def ff2(g, tg, hs):
            s = g % NSLOT
            for bi in range(4):
                b = tg * 4 + bi
                xb = x1[:, b, :]
                for hf in range(2):
                    bank = cntg[0] % 3
                    cntg[0] += 1

                    def f(e, bank=bank, hf=hf, bi=bi):
                        last = None
                        for fc in range(4):
                            last = e.matmul(pg[bank][:, :], hT[:, hs, fc, bi * 128:(bi + 1) * 128],
                                            Wf2[s][:, fc, hf * 512:(hf + 1) * 512], start=(fc == 0), stop=(fc == 3))
                        return last
                    S.op("pe", [("Wf2", s)] + [("hT", hs, fc) for fc in range(4)], [("ps", bank)], f)
                    if g == 0:
                        S.op("dve", [("ps", bank), ("x1", b)], [("x1", b)], lambda e, bank=bank, hf=hf, xb=xb:
                             e.scalar_tensor_tensor(out=xb[:, hf * 512:(hf + 1) * 512],
                                                    in0=xb[:, hf * 512:(hf + 1) * 512], scalar=ALPHA,
                                                    in1=pg[bank][:, :], op0=ALU.mult, op1=ALU.add))
                    else:
                        S.op("dve", [("ps", bank), ("x1", b)], [("x1", b)], lambda e, bank=bank, hf=hf, xb=xb:
                             e.tensor_tensor(out=xb[:, hf * 512:(hf + 1) * 512], in0=pg[bank][:, :],
                                             in1=xb[:, hf * 512:(hf + 1) * 512], op=ALU.add))
                if g == NG - 1:
                    if len(pend) >= 2:
                        ln2_tail(pend.pop(0))
                    ln_stats(xb, ("x1", b), b % NSS)
                    pend.append(b)

        def ff2pair(tg, sa, sb):
            for bi in range(4):
                b = tg * 4 + bi
                xb = x1[:, b, :]
                for hf in range(2):
                    bank = cntg[0] % 3
                    cntg[0] += 1

                    def f(e, bank=bank, hf=hf, bi=bi):
                        last = None
                        n = 0
                        for ws, hs in ((0, sa), (1, sb)):
                            for fc in range(4):
                                last = e.matmul(pg[bank][:, :], hT[:, hs, fc, bi * 128:(bi + 1) * 128],
                                                Wf2[ws][:, fc, hf * 512:(hf + 1) * 512],
                                                start=(n == 0), stop=(n == 7))
                                n += 1
                        return last
                    S.op("pe", [("Wf2", 0), ("Wf2", 1)] + [("hT", sa, fc) for fc in range(4)]
                         + [("hT", sb, fc) for fc in range(4)], [("ps", bank)], f)
                    S.op("dve", [("ps", bank), ("x1", b)], [("x1", b)], lambda e, bank=bank, hf=hf, xb=xb:
                         e.tensor_tensor(out=xb[:, hf * 512:(hf + 1) * 512], in0=pg[bank][:, :],
                                         in1=xb[:, hf * 512:(hf + 1) * 512], op=ALU.add))
                ln_stats(xb, ("x1", b), b % NSS)
                pend.append(b)
                if len(pend) > 2:
                    ln2_tail(pend.pop(0))

        pend = []

        def ln2_tail(b):
            xb = x1[:, b, :]
            si = b % NSS
            if b % 2 == 1 or b >= NB - 3:
                S.op("dve", [("x1", b), ("mv", si), "l1g"], [("x1", b)], lambda e: e.scalar_tensor_tensor(
                    out=xb, in0=xb, scalar=mv[:, si, 0:1], in1=l1g[:], op0=ALU.subtract, op1=ALU.mult))
                S.op("dve", [("x1", b), ("rstd", si), "l1b"], [("x1", b)], lambda e: e.scalar_tensor_tensor(
                    out=xb, in0=xb, scalar=rstd[:, si:si + 1], in1=l1b[:], op0=ALU.mult, op1=ALU.add))
            else:
                ln_apply(xb, (l1g[:], "l1g"), (l1b[:], "l1b"), ("x1", b), "pool", "act", si)
            S.dma("sp", out_d[b * 128:(b + 1) * 128, :], xb, [("x1", b)], [])

        def reload_ln():
            S.dma("sp", l1g[:], l2g_d, [], ["l1g"])
            S.dma("sp", l1b[:], l2b_d, [], ["l1b"])

        seq = [(g, tg) for g in range(NG - 2) for tg in range(4)]
        p2ops = [lambda: ff1(seq[0][0], seq[0][1], 0)]
        for i, (g, tg) in enumerate(seq):
            if i + 1 < len(seq):
                p2ops.append(lambda i=i: ff1(seq[i + 1][0], seq[i + 1][1], (i + 1) % 2))
                if seq[i + 1] == (0, 3):
                    p2ops.append(reload_ln)
            else:
                p2ops.append(lambda: ff1(NG - 2, 0, 0))
            p2ops.append(lambda i=i, g=g, tg=tg: ff2(g, tg, i % 2))
            if tg == 3 and g + NSLOT < NG:
                p2ops.append(lambda g=g: load_w(g + NSLOT))
        GA, GB = NG - 2, NG - 1
        p2ops += [lambda: ff1(GB, 0, 1), lambda: ff1(GA, 1, 2), lambda: ff2pair(0, 0, 1),
                  lambda: ff1(GB, 1, 0), lambda: ff1(GA, 2, 1), lambda: ff2pair(1, 2, 0),
                  lambda: ff1(GB, 2, 2), lambda: ff1(GA, 3, 0), lambda: ff2pair(2, 1, 2),
                  lambda: ff1(GB, 3, 1), lambda: ff2pair(3, 0, 1)]
        p2ops.reverse()
        halo()
        for it in range(NB + 5):
            if 0 <= it - 2 < NB:
                step_E(it - 2)
            if it == NB + 1:
                for bi_ in range(4):
                    ffn_init(bi_)
                p2ops.pop()()
            if it < NB:
                A_pe(it)
            if it == 0:
                halo_b()
            if it == 1:
                step_C(0)
            if 0 <= it - 2 < NB and it < NB:
                F_pe(it - 2)
                F_evac(it - 2)
            if 0 <= it - 1 < NB:
                D_all(it - 1)
                D_exp(it - 1, 0)
                B_pe(it - 1)
                D_exp(it - 1, 1)
            if 0 <= it - 2 < NB and it >= NB:
                F_pe(it - 2)
                F_evac(it - 2)
            if it == 1:
                step_U(0)
            if it < NB:
                rope_k(it + 1, it % 2)
            if 0 <= it - 1 < NB:
                B_pool(it - 1)
            if it < NB:
                rope_q(it + 1, it % 2)
            if 0 <= it - 4 < NB:
                G_tail(it - 4)
            if it < NB:
                A_vstats(it)
            if 0 <= it - 3 < NB:
                G_pe(it - 3)
            if 0 < it < NB:
                step_C(it)
            if 0 <= it - 4 < NB:
                step_H(it - 4)
            if it < NB:
                A_vtail(it)
            if 0 <= it - 4 < NB:
                G_late_act(it - 4)
            late_loads(it)
            if it % 4 == 0 and 0 < it < NB:
                step_U(it // 4)
            if it == NB - 1:
                S.alias([("Wf1", 0), ("Wf2", 0)] + [("hT", s_, f_) for s_ in range(3) for f_ in range(4)], WIN + WU)
                load_w(0)
            if it == NB:
                S.alias([("hf32", 0), ("hf32", 1)], [("uT", j) for j in range(4)])
                p2ops.pop()()
            if it == NB + 1:
                p2ops.pop()()
            if it == NB + 2:
                S.alias([("Wf1", 1), ("Wf2", 1)], WOUT)
                load_w(1)
                p2ops.pop()()
                p2ops.pop()()
            if it == NB + 3:
                p2ops.pop()()
        while p2ops:
            p2ops.pop()()
        while pend:
            ln2_tail(pend.pop(0))
        S.drain("sp")
    return nc


_CACHE = {}


def kernel(x, positions, w_in, v_ln_g, v_ln_b, w_spatial, b_spatial, sinks, w_out,
           ln1_g, ln1_b, w_ff1, w_ff2, ln2_g, ln2_b):
    f32 = np.float32
    x = np.asarray(x, f32)
    positions = np.asarray(positions, np.int32)
    B, SEQ, _ = x.shape

    def rep(v, n):
        return np.ascontiguousarray(np.broadcast_to(np.asarray(v, f32).reshape(1, n), (128, n)))

    s_idx = np.arange(128)[:, None]
    q_idx = np.arange(128)[None, :]
    m_prev = (s_idx > q_idx).astype(f32)
    m_cur = (s_idx <= q_idx).astype(f32)
    inv_freq64 = 10000.0 ** (-np.arange(0, 64, 2, dtype=np.float64) / 64.0)
    inv_freq = inv_freq64.astype(f32)
    inv_freq_lo = (inv_freq64 - inv_freq.astype(np.float64)).astype(f32)
    common = {
        "w_in": np.ascontiguousarray(np.asarray(w_in, f32)[0]),
        "w_out": np.ascontiguousarray(np.asarray(w_out, f32)[0]),
        "w_ff1": np.ascontiguousarray(np.asarray(w_ff1, f32)[0]),
        "w_ff2": np.ascontiguousarray(np.asarray(w_ff2, f32)[0]),
        "w_spT": np.ascontiguousarray(np.asarray(w_spatial, f32)[0].transpose(2, 0, 1)),
        "bT": np.ascontiguousarray(np.broadcast_to(
            np.asarray(b_spatial, f32)[0].reshape(4, 2, 1, 128).transpose(1, 2, 0, 3), (2, 64, 4, 128)
        ).reshape(128, 4, 128)),
        "gcol": np.ascontiguousarray(np.asarray(v_ln_g, f32)[0].reshape(4, 2, 64).transpose(1, 2, 0).reshape(128, 4)),
        "bcol": np.ascontiguousarray(np.asarray(v_ln_b, f32)[0].reshape(4, 2, 64).transpose(1, 2, 0).reshape(128, 4)),
        "g1col": np.ascontiguousarray(np.asarray(ln1_g, f32)[0].reshape(8, 128).T),
        "b1col": np.ascontiguousarray(np.asarray(ln1_b, f32)[0].reshape(8, 128).T),
        "l1g": rep(np.asarray(ln1_g)[0], D),
        "l1b": rep(np.asarray(ln1_b)[0], D),
        "l2g": rep(np.asarray(ln2_g)[0], D),
        "l2b": rep(np.asarray(ln2_b)[0], D),
        "esk": rep(np.asarray(sinks)[0], 8),
        "ident": np.eye(128, dtype=f32),
        "invf": rep(inv_freq, 32),
        "invf_lo": rep(inv_freq_lo, 32),
    }
    in_maps = []
    for c in range(NCORES):
        b, h = divmod(c, 2)
        t0 = h * T
        xs = x[b, t0:t0 + T]
        xT = np.zeros((D, T + 128), f32)
        xT[:, 128:] = xs.T
        pos = np.zeros((NB + 1) * 128, np.int32)
        pos[128:] = positions[b, t0:t0 + T]
        msk = np.zeros((128, 2, 2, 128), f32)
        msk[:, 1, 0] = m_prev
        msk[:, 1, 1] = m_cur
        msk[:, 0, 1] = m_cur
        if h == 1:
            xT[:, :128] = x[b, t0 - 128:t0].T
            pos[:128] = positions[b, t0 - 128:t0]
            msk[:, 0, 0] = m_prev
        mb = np.where(msk > 0.5, f32(0.0), f32(-30000.0)).astype(f32)
        m = dict(common)
        m["mbias"] = np.ascontiguousarray(np.broadcast_to(mb[:, :, :, None, :], (128, 2, 2, 4, 128)).reshape(128, 2, 2, 512))
        m["x"] = np.ascontiguousarray(xs)
        m["xT"] = xT
        m["pos"] = np.ascontiguousarray(pos.reshape(NB + 1, 128).T)
        m["masks"] = msk
        in_maps.append(m)

    if "nc" not in _CACHE:
        _CACHE["nc"] = build_program()
    res = run_bass_kernel_spmd(_CACHE["nc"], in_maps, core_ids=list(range(NCORES)))
    out = np.empty((B, SEQ, D), f32)
    for c in range(NCORES):
        b, h = divmod(c, 2)
        out[b, h * T:(h + 1) * T] = res.results[c]["out"]
    return out
```
